# Optimizing a Trainium2 kernel written in Bass

```python
import math
import jax, jax.numpy as jnp
from jax import lax
import numpy as np

D_MODEL = 1024
BATCH = 8
SEQ = 4096
DEPTH = 1

MEM_LEN = 256
DIFF_HEADS = 8
DIFF_HEAD_DIM = 64
DIFF_V_DIM = 2 * DIFF_HEAD_DIM
DIFF_QK_WIDTH = DIFF_HEADS * 2 * DIFF_HEAD_DIM
DIFF_WIDTH = DIFF_HEADS * DIFF_V_DIM
CONV_WIDTH = D_MODEL
CONV_K = 3
MEM_HEADS = 4
MEM_HEAD_DIM = 256
MEM_WIDTH = MEM_HEADS * MEM_HEAD_DIM
N_BRANCH = 3
D_FF = 4 * D_MODEL
Q_BLOCK = 128
NORM_EPS = 1e-6
MASK_VALUE = -1e30

SPLIT_SIZES = (DIFF_QK_WIDTH, DIFF_QK_WIDTH, DIFF_WIDTH, 3 * CONV_WIDTH, MEM_WIDTH, N_BRANCH * D_MODEL)
SPLIT_POINTS = tuple(int(v) for v in np.cumsum(SPLIT_SIZES)[:-1])
PROJ_WIDTH = int(sum(SPLIT_SIZES))

kernel_name = "hybrid_diffattn_shortconv_memxattn_block"


def rms_norm(x, g):
    xf = x.astype(jnp.float32)
    y = xf * lax.rsqrt(jnp.mean(xf * xf, axis=-1, keepdims=True) + NORM_EPS)
    return (y * g.astype(jnp.float32)).astype(x.dtype)


def diff_attention(q_d, k_d, v_d, lam, q_norm_g, k_norm_g):
    B, S = q_d.shape[0], q_d.shape[1]
    q = rms_norm(q_d.reshape(B, S, DIFF_HEADS, 2, DIFF_HEAD_DIM), q_norm_g)
    q = q * jnp.asarray(DIFF_HEAD_DIM ** -0.5, q.dtype)
    k = rms_norm(k_d.reshape(B, S, DIFF_HEADS, 2, DIFF_HEAD_DIM), k_norm_g)
    v = v_d.reshape(B, S, DIFF_HEADS, DIFF_V_DIM)
    nb = S // Q_BLOCK
    q_blocks = jnp.moveaxis(q.reshape(B, nb, Q_BLOCK, DIFF_HEADS, 2, DIFF_HEAD_DIM), 1, 0)
    key_pos = jnp.arange(S)

    def one_block(args):
        q_blk, blk = args
        s = jnp.einsum('bqhcd,bkhcd->bhcqk', q_blk, k).astype(jnp.float32)
        q_pos = blk * Q_BLOCK + jnp.arange(Q_BLOCK)
        causal = key_pos[None, :] <= q_pos[:, None]
        s = jnp.where(causal, s, MASK_VALUE)
        p = jax.nn.softmax(s, axis=-1)
        a = p[:, :, 0] - lam * p[:, :, 1]
        return jnp.einsum('bhqk,bkhe->bqhe', a.astype(v.dtype), v)

    o = lax.map(one_block, (q_blocks, jnp.arange(nb)))
    return jnp.moveaxis(o, 0, 1).reshape(B, S, DIFF_HEADS, DIFF_V_DIM)


def short_gated_conv(conv_in, conv_w):
    x_c, gate_b, gate_c = jnp.split(conv_in, 3, axis=-1)
    inner = gate_c * x_c
    y = lax.conv_general_dilated(
        inner, conv_w[:, None, :], window_strides=(1,), padding=[(CONV_K - 1, 0)],
        dimension_numbers=('NWC', 'WIO', 'NWC'), feature_group_count=CONV_WIDTH)
    return gate_b * y


def memory_attention(q_m, mem_n, w_mem_kv, mq_norm_g, mk_norm_g):
    B, S = q_m.shape[0], q_m.shape[1]
    M = mem_n.shape[1]
    kv = mem_n @ w_mem_kv
    k_m, v_m = jnp.split(kv, 2, axis=-1)
    q = rms_norm(q_m.reshape(B, S, MEM_HEADS, MEM_HEAD_DIM), mq_norm_g)
    q = q * jnp.asarray(MEM_HEAD_DIM ** -0.5, q.dtype)
    k = rms_norm(k_m.reshape(B, M, MEM_HEADS, MEM_HEAD_DIM), mk_norm_g)
    v = v_m.reshape(B, M, MEM_HEADS, MEM_HEAD_DIM)
    s = jnp.einsum('bshd,bmhd->bhsm', q, k).astype(jnp.float32)
    p = jax.nn.softmax(s, axis=-1)
    o = jnp.einsum('bhsm,bmhd->bshd', p.astype(v.dtype), v)
    return o.reshape(B, S, MEM_WIDTH)


def hybrid_layer(x, mem, lam_init, norm_mix_g, norm_mem_g, w_in, b_gate, q_norm_g, k_norm_g,
                 lam_q1, lam_k1, lam_q2, lam_k2, subln_g, w_attn_o, conv_w, w_conv_o,
                 w_mem_kv, mq_norm_g, mk_norm_g, w_mem_o, w_o, norm_mlp_g, w_mlp_in, w_mlp_out):
    B, S, D = x.shape
    h = rms_norm(x, norm_mix_g)
    proj = h @ w_in
    q_d, k_d, v_d, conv_in, q_m, gate_logits = jnp.split(proj, SPLIT_POINTS, axis=-1)

    lam = (jnp.exp(jnp.sum(lam_q1.astype(jnp.float32) * lam_k1.astype(jnp.float32)))
           - jnp.exp(jnp.sum(lam_q2.astype(jnp.float32) * lam_k2.astype(jnp.float32)))
           + lam_init)
    o = diff_attention(q_d.reshape(B, S, DIFF_HEADS, -1), k_d.reshape(B, S, DIFF_HEADS, -1),
                       v_d.reshape(B, S, DIFF_HEADS, -1), lam, q_norm_g, k_norm_g)
    o = rms_norm(o, subln_g) * jnp.asarray(1.0 - lam_init, o.dtype)
    y_attn = o.reshape(B, S, DIFF_WIDTH) @ w_attn_o

    y_conv = short_gated_conv(conv_in, conv_w) @ w_conv_o

    mem_n = rms_norm(mem, norm_mem_g)
    y_mem = memory_attention(q_m, mem_n, w_mem_kv, mq_norm_g, mk_norm_g) @ w_mem_o

    g = jax.nn.sigmoid((gate_logits + b_gate).astype(jnp.float32)).astype(x.dtype)
    g = g.reshape(B, S, N_BRANCH, D)
    merged = g[:, :, 0] * y_attn + g[:, :, 1] * y_conv + g[:, :, 2] * y_mem
    x = x + merged @ w_o

    h2 = rms_norm(x, norm_mlp_g)
    x = x + jnp.square(jax.nn.relu(h2 @ w_mlp_in)) @ w_mlp_out
    return x


def setup_inputs(seed: int = 0) -> dict:
    key = jax.random.key(seed)
    ks = jax.random.split(key, 26)
    f32 = jnp.float32
    L = DEPTH

    def nrm(k, shape, scale):
        return jax.random.normal(k, shape, f32) * scale

    def gain(k, shape):
        return 1.0 + 0.05 * jax.random.normal(k, shape, f32)

    return {
        'x': nrm(ks[0], (BATCH, SEQ, D_MODEL), 1.0),
        'mem': nrm(ks[1], (BATCH, MEM_LEN, D_MODEL), 1.0),
        'norm_mix_g': gain(ks[2], (L, D_MODEL)),
        'norm_mem_g': gain(ks[3], (L, D_MODEL)),
        'w_in': nrm(ks[4], (L, D_MODEL, PROJ_WIDTH), D_MODEL ** -0.5),
        'b_gate': nrm(ks[5], (L, N_BRANCH * D_MODEL), 0.1),
        'q_norm_g': gain(ks[6], (L, DIFF_HEAD_DIM)),
        'k_norm_g': gain(ks[7], (L, DIFF_HEAD_DIM)),
        'lam_q1': nrm(ks[8], (L, DIFF_HEAD_DIM), 0.1),
        'lam_k1': nrm(ks[9], (L, DIFF_HEAD_DIM), 0.1),
        'lam_q2': nrm(ks[10], (L, DIFF_HEAD_DIM), 0.1),
        'lam_k2': nrm(ks[11], (L, DIFF_HEAD_DIM), 0.1),
        'subln_g': gain(ks[12], (L, DIFF_V_DIM)),
        'w_attn_o': nrm(ks[13], (L, DIFF_WIDTH, D_MODEL), DIFF_WIDTH ** -0.5),
        'conv_w': nrm(ks[14], (L, CONV_K, CONV_WIDTH), CONV_K ** -0.5),
        'w_conv_o': nrm(ks[15], (L, CONV_WIDTH, D_MODEL), CONV_WIDTH ** -0.5),
        'w_mem_kv': nrm(ks[16], (L, D_MODEL, 2 * MEM_WIDTH), D_MODEL ** -0.5),
        'mq_norm_g': gain(ks[17], (L, MEM_HEAD_DIM)),
        'mk_norm_g': gain(ks[18], (L, MEM_HEAD_DIM)),
        'w_mem_o': nrm(ks[19], (L, MEM_WIDTH, D_MODEL), MEM_WIDTH ** -0.5),
        'w_o': nrm(ks[20], (L, D_MODEL, D_MODEL), D_MODEL ** -0.5),
        'norm_mlp_g': gain(ks[21], (L, D_MODEL)),
        'w_mlp_in': nrm(ks[22], (L, D_MODEL, D_FF), D_MODEL ** -0.5),
        'w_mlp_out': nrm(ks[23], (L, D_FF, D_MODEL), D_FF ** -0.5),
    }


def reference(x, mem, norm_mix_g, norm_mem_g, w_in, b_gate, q_norm_g, k_norm_g,
              lam_q1, lam_k1, lam_q2, lam_k2, subln_g, w_attn_o, conv_w, w_conv_o,
              w_mem_kv, mq_norm_g, mk_norm_g, w_mem_o, w_o, norm_mlp_g, w_mlp_in, w_mlp_out):
    for l in range(DEPTH):
        lam_init = 0.8 - 0.6 * math.exp(-0.3 * l)
        x = hybrid_layer(x, mem, lam_init, norm_mix_g[l], norm_mem_g[l], w_in[l], b_gate[l],
                         q_norm_g[l], k_norm_g[l], lam_q1[l], lam_k1[l], lam_q2[l], lam_k2[l],
                         subln_g[l], w_attn_o[l], conv_w[l], w_conv_o[l], w_mem_kv[l],
                         mq_norm_g[l], mk_norm_g[l], w_mem_o[l], w_o[l], norm_mlp_g[l],
                         w_mlp_in[l], w_mlp_out[l])
    return x
```

```python
import numpy as np
from contextlib import ExitStack
import concourse.bass as bass
import concourse.mybir as mybir
from concourse.bass_utils import run_bass_kernel_spmd

F32 = mybir.dt.float32
BF16 = mybir.dt.bfloat16
AF = mybir.ActivationFunctionType
ALU = mybir.AluOpType
AX = mybir.AxisListType

ENGS = ["pe", "act", "dve", "pool", "sp"]
BLOCKNAME = {"pe": "tensor", "act": "scalar", "dve": "vector", "pool": "gpsimd", "sp": "sync"}
N_DMA_SEMS = {"sp": 32, "pool": 16}

S = 4096
D = 1024
NSLOT = 186
EPS = 1e-6


class Res:
    __slots__ = ("name", "w", "r")

    def __init__(self, name=""):
        self.name = name
        self.w = None
        self.r = []


class Op:
    __slots__ = ("eng", "fn", "deps", "pos", "is_dma", "dsem", "dval", "mark", "markidx",
                 "clock", "waits", "label")


class Prog:
    def __init__(self, nc, es):
        self.nc = nc
        self.ops = {e: [] for e in ENGS}
        self.all = []
        self.esem = {e: es.enter_context(nc.semaphore("s_" + e)) for e in ENGS}
        self.dsems = {q: [es.enter_context(nc.semaphore("d_%s_%d" % (q, i))) for i in range(n)]
                      for q, n in N_DMA_SEMS.items()}
        self.dcnt = {q: [0] * n for q, n in N_DMA_SEMS.items()}
        self.dlast = {q: [None] * n for q, n in N_DMA_SEMS.items()}
        self.drr = {q: 0 for q in N_DMA_SEMS}
        self.label = ""

    def op(self, eng, fn, reads=(), writes=(), deps=(), dma=False):
        o = Op()
        o.eng = eng
        o.fn = fn
        o.is_dma = dma
        o.mark = False
        o.markidx = None
        o.dsem = None
        o.dval = None
        o.label = self.label
        d = {}
        for x in deps:
            if x is not None:
                d[x] = True
        for r in reads:
            if r.w is not None:
                d[r.w] = True
        for w in writes:
            if w.w is not None:
                d[w.w] = True
            for x in w.r:
                if x not in d:
                    d[x] = False
        for r in reads:
            r.r.append(o)
        for w in writes:
            w.w = o
            w.r = []
        if dma:
            q = eng
            i = self.drr[q]
            self.drr[q] = (i + 1) % len(self.dsems[q])
            prev = self.dlast[q][i]
            if prev is not None:
                d[prev] = True
            self.dcnt[q][i] += 1
            o.dsem = self.dsems[q][i]
            o.dval = 16 * self.dcnt[q][i]
            self.dlast[q][i] = o
        d.pop(o, None)
        o.deps = d
        o.pos = len(self.ops[eng])
        self.ops[eng].append(o)
        self.all.append(o)
        return o

    def finalize(self):
        ck = {e: {f: -1 for f in ENGS} for e in ENGS}
        kd = {e: set() for e in ENGS}
        for o in self.all:
            e = o.eng
            c = ck[e]
            best = {}
            dwaits = []
            for d, strong in o.deps.items():
                if d.is_dma:
                    if d not in kd[e]:
                        kd[e].add(d)
                        dwaits.append(d)
                    continue
                if d.eng == e:
                    if e in ("pe", "sp"):
                        continue
                if c[d.eng] >= d.pos:
                    continue
                b = best.get(d.eng)
                if b is None or d.pos > b.pos:
                    best[d.eng] = d
            waits = []
            for d in dwaits:
                for f in ENGS:
                    if d.clock[f] > c[f]:
                        c[f] = d.clock[f]
                waits.append(d)
            for f, d in best.items():
                if c[f] >= d.pos:
                    continue
                d.mark = True
                for g in ENGS:
                    if d.clock[g] > c[g]:
                        c[g] = d.clock[g]
                if d.pos > c[f]:
                    c[f] = d.pos
                waits.append(d)
            o.waits = waits
            o.clock = dict(c)
        self.nmarks = {}
        for e in ENGS:
            n = 0
            for o in self.ops[e]:
                if o.mark:
                    n += 1
                    o.markidx = n
            self.nmarks[e] = n

    def emit(self, block):
        for e in ENGS:
            if not self.ops[e]:
                continue

            def body(eng, e=e):
                for o in self.ops[e]:
                    for d in o.waits:
                        if d.is_dma:
                            eng.wait_ge(d.dsem, d.dval)
                        else:
                            eng.wait_ge(self.esem[d.eng], d.markidx)
                    if o.fn is None:
                        continue
                    ins = o.fn(eng)
                    if o.is_dma:
                        ins.then_inc(o.dsem, 16)
                    elif o.mark:
                        ins.then_inc(self.esem[e], 1)

            getattr(block, BLOCKNAME[e])(body)

    def stats(self):
        return {e: (len(self.ops[e]), self.nmarks.get(e)) for e in ENGS}


import os
SKIP = set(os.environ.get("MK_SKIP", "").split(","))


def build(nheads=8, nqc=8, nchunks=8, dbg=False):
    nc = bass.Bass("TRN2", target_bir_lowering=False)

    def din(name, shape):
        return nc.dram_tensor(name, list(shape), F32, kind="ExternalInput").ap()

    x = din("x", [S, D])
    mem = din("mem", [256, D])
    norm_mix_g = din("norm_mix_g", [D])
    norm_mem_g = din("norm_mem_g", [D])
    w_in = din("w_in", [D, 10240])
    b_gate = din("b_gate", [3072])
    q_norm_g = din("q_norm_g", [64])
    k_norm_g = din("k_norm_g", [64])
    lam_q1 = din("lam_q1", [64])
    lam_k1 = din("lam_k1", [64])
    lam_q2 = din("lam_q2", [64])
    lam_k2 = din("lam_k2", [64])
    subln_g = din("subln_g", [128])
    w_attn_o = din("w_attn_o", [D, D])
    conv_w = din("conv_w", [3, D])
    w_conv_o = din("w_conv_o", [D, D])
    w_mem_kv = din("w_mem_kv", [D, 2048])
    mq_norm_g = din("mq_norm_g", [256])
    mk_norm_g = din("mk_norm_g", [256])
    w_mem_o = din("w_mem_o", [D, D])
    w_o = din("w_o", [D, D])
    norm_mlp_g = din("norm_mlp_g", [D])
    w_mlp_in = din("w_mlp_in", [D, 4096])
    w_mlp_out = din("w_mlp_out", [4096, D])
    consts = din("consts", [128, 768])
    out = nc.dram_tensor("out", [S, D], F32, kind="ExternalOutput").ap()

    okind = "ExternalOutput" if dbg else "Internal"
    sc_in = nc.dram_tensor("sc_in", [D, 7168], BF16, kind="Internal").ap()
    sc_sq = [nc.dram_tensor("sc_sq%d" % i, [D, D], BF16, kind="Internal").ap() for i in range(4)]
    sc_w1 = nc.dram_tensor("sc_w1", [D, 4096], BF16, kind="Internal").ap()
    sc_w2 = nc.dram_tensor("sc_w2", [4096, D], BF16, kind="Internal").ap()
    oTd = nc.dram_tensor("oTd", [D, S], BF16, kind=okind).ap()

    es = ExitStack()
    with es:
        P = Prog(nc, es)

        def sbt(name, shape, dt):
            return es.enter_context(nc.sbuf_tensor(name, shape, dt))

        AB = sbt("arena", [128, NSLOT * 512], BF16)
        RS = [Res("s%d" % i) for i in range(NSLOT)]
        identF = sbt("identF", [128, 128], F32)
        cb = sbt("cb", [128, 640], BF16)
        vecs = sbt("vecs", [128, 64], F32)
        sc = sbt("sc", [128, 64], F32)
        gmix = sbt("gmix", [128, D], F32)
        gmlp = sbt("gmlp", [128, D], F32)
        kmT = sbt("kmT", [128, 8 * 256], BF16)
        vm = sbt("vm", [128, 2 * D], BF16)
        halo = sbt("halo", [128, 16], F32)
        rsd = sbt("rsd", [128, 64], F32)
        epi = sbt("epi", [128, 32], F32)
        gsubb = sbt("gsubb", [128, 128], F32)
        R_epi = [Res("epi%d" % i) for i in range(4)]
        R_gsubb = Res("gsubb")
        R_ident, R_cb, R_vecs, R_sc, R_gmix, R_gmlp, R_kmT, R_vm = (Res(n) for n in
            ["ident", "cb", "vecs", "sc", "gmix", "gmlp", "kmT", "vm"])
        R_halo = [Res("halo%d" % i) for i in range(8)]
        R_rsd = [Res("rsd%d" % i) for i in range(64)]
        R_scc = [Res("scc%d" % i) for i in range(64)]
        psall = es.enter_context(nc.psum_tensor("psall", [128, 4096], F32))
        ps = [psall[:, i * 512:(i + 1) * 512] for i in range(8)]
        RP = [Res("ps%d" % i) for i in range(8)]

        tri = cb[:, 0:128]
        blk = cb[:, 128:256]
        ones = cb[:, 256:384]
        identB = cb[:, 384:512]
        negtri = cb[:, 512:640]

        def Bv(i, a=0, b=512):
            return AB[:, i * 512 + a:i * 512 + b]

        def Fv(i, a=0, b=512):
            return AB[:, i * 512:(i + 2) * 512].bitcast(F32)[:, a:b]

        def F4(i, a=0, b=1024):
            return AB[:, i * 512:(i + 4) * 512].bitcast(F32)[:, a:b]

        def rs(i, n=1):
            return RS[i:i + n]

        def dma(q, out_ap, in_ap, reads=(), writes=(), deps=()):
            return P.op(q, lambda e: e.dma_start(out=out_ap, in_=in_ap), reads, writes, deps, dma=True)

        def mm(out_ap, lhsT, rhs, start, stop, reads, writes, sgc=False):
            return P.op("pe", lambda e: e.matmul(out_ap, lhsT=lhsT, rhs=rhs, start=start, stop=stop,
                                                 skip_group_check=sgc), reads, writes)

        def tr(out_ap, in_ap, ident_ap, reads, writes):
            return P.op("pe", lambda e: e.transpose(out=out_ap, in_=in_ap, identity=ident_ap), reads, writes)

        def act(out_ap, in_ap, func, reads, writes, bias=None, scale=None, accum=None):
            kw = {}
            if bias is not None:
                kw["bias"] = bias
            if scale is not None:
                kw["scale"] = scale
            if accum is not None:
                kw["accum_out"] = accum
            return P.op("act", lambda e: e.activation(out=out_ap, in_=in_ap, func=func, **kw), reads, writes)

        def tcopy(eng, out_ap, in_ap, reads, writes):
            return P.op(eng, lambda e: e.tensor_copy(out=out_ap, in_=in_ap), reads, writes)

        def tt(eng, out_ap, a, b, op, reads, writes):
            return P.op(eng, lambda e: e.tensor_tensor(out=out_ap, in0=a, in1=b, op=op), reads, writes)

        def tsc(eng, out_ap, a, s1, op0, reads, writes, s2=None, op1=None):
            if op1 is None:
                return P.op(eng, lambda e: e.tensor_scalar(out=out_ap, in0=a, scalar1=s1, scalar2=None, op0=op0),
                            reads, writes)
            return P.op(eng, lambda e: e.tensor_scalar(out=out_ap, in0=a, scalar1=s1, scalar2=s2, op0=op0, op1=op1),
                        reads, writes)

        def stt(out_ap, a, s, b, op0, op1, reads, writes):
            return P.op("dve", lambda e: e.scalar_tensor_tensor(out=out_ap, in0=a, scalar=s, in1=b, op0=op0, op1=op1),
                        reads, writes)

        def memset(eng, ap, val, writes):
            return P.op(eng, lambda e: e.memset(ap, val), (), writes)

        class Ring:
            def __init__(self, items):
                self.items = list(items)
                self.i = 0

            def next(self):
                v = self.items[self.i % len(self.items)]
                self.i += 1
                return v

        out_dmas = []

        dma("sp", identF[:], consts[:, 0:128], writes=[R_ident])
        dma("pool", cb[:], consts[:, 128:768], writes=[R_cb])
        dma("sp", gmix[:], norm_mix_g.partition_broadcast(128), writes=[R_gmix])
        dma("sp", gmlp[:], norm_mlp_g.partition_broadcast(128), writes=[R_gmlp])
        memset("dve", sc[:, 0:1], EPS, [R_scc[0]])
        dma("sp", gsubb[:], subln_g.partition_broadcast(128), writes=[R_gsubb])
        tsc("dve", gsubb[:], gsubb[:], 0.8, ALU.mult, [R_gsubb], [R_gsubb])
        memset("dve", halo[:], 0.0, R_halo)

        STG = 162
        stg_full = AB[:, STG * 512:(STG + 1) * 512].bitcast(F32)
        memset("dve", stg_full[:, 0:128], 0.0, rs(STG))

        def stg_rows(r0, r1, c0=0, c1=128):
            return AB[r0:r1, STG * 512:(STG + 1) * 512].bitcast(F32)[:, c0:c1]

        dma("sp", stg_rows(0, 24), b_gate.rearrange("(r c) -> r c", c=128), writes=rs(STG))
        dma("sp", stg_rows(24, 48), conv_w.rearrange("j (c p) -> (j c) p", p=128), writes=rs(STG))
        dma("sp", stg_rows(48, 50), mq_norm_g.rearrange("(r c) -> r c", c=128), writes=rs(STG))
        dma("sp", stg_rows(50, 52), mk_norm_g.rearrange("(r c) -> r c", c=128), writes=rs(STG))
        dma("sp", stg_rows(52, 53, 0, 64), q_norm_g.rearrange("(o n) -> o n", o=1), writes=rs(STG))
        dma("sp", stg_rows(52, 53, 64, 128), q_norm_g.rearrange("(o n) -> o n", o=1), writes=rs(STG))
        dma("sp", stg_rows(53, 54, 0, 64), k_norm_g.rearrange("(o n) -> o n", o=1), writes=rs(STG))
        dma("sp", stg_rows(53, 54, 64, 128), k_norm_g.rearrange("(o n) -> o n", o=1), writes=rs(STG))
        dma("sp", stg_rows(54, 55), subln_g.rearrange("(o n) -> o n", o=1), writes=rs(STG))
        tr(ps[0][:, 0:64], stg_rows(0, 64), identF[0:64, 0:64], rs(STG) + [R_ident], [RP[0]])
        tcopy("dve", vecs[:], ps[0][:, 0:64], [RP[0]], [R_vecs])
        tsc("dve", vecs[:, 55:56], vecs[:, 52:53], 0.125, ALU.mult, [R_vecs], [R_vecs])
        tsc("dve", vecs[:, 56:57], vecs[:, 54:55], 0.8, ALU.mult, [R_vecs], [R_vecs])
        tsc("dve", vecs[:, 57:59], vecs[:, 48:50], 1.0 / 16.0, ALU.mult, [R_vecs], [R_vecs])
        V_BG = lambda n: vecs[:, n:n + 1]
        V_CW = lambda j, c: vecs[:, 24 + j * 8 + c:25 + j * 8 + c]
        V_MKG = lambda dc: vecs[:, 50 + dc:51 + dc]
        V_MQG16 = lambda dc: vecs[:, 57 + dc:58 + dc]
        V_GK = vecs[:, 53:54]
        V_GQ8 = vecs[:, 55:56]
        V_SUB08 = vecs[:, 56:57]
        EPSC = sc[:, 0:1]
        LAM = sc[:, 1:2]
        NEGLAM = sc[:, 2:3]
        NEGM = sc[:, 3:4]
        NEGMM = sc[:, 4:5]

        BC = 156
        bc = AB[:, BC * 512:(BC + 2) * 512].bitcast(F32)
        for i, v in enumerate([lam_q1, lam_k1, lam_q2, lam_k2, q_norm_g, k_norm_g]):
            dma("sp", bc[:, i * 64:(i + 1) * 64], v.partition_broadcast(128), writes=rs(BC, 2))
        MB = 158
        mb = AB[:, MB * 512:(MB + 4) * 512].bitcast(F32)
        dma("sp", mb[:, 0:256], mq_norm_g.partition_broadcast(128), writes=rs(MB, 4))
        dma("sp", mb[:, 256:512], mk_norm_g.partition_broadcast(128), writes=rs(MB, 4))
        tt("dve", bc[:, 384:448], bc[:, 0:64], bc[:, 64:128], ALU.mult, rs(BC, 2), rs(BC, 2))
        tt("dve", bc[:, 448:512], bc[:, 128:192], bc[:, 192:256], ALU.mult, rs(BC, 2), rs(BC, 2))
        P.op("dve", lambda e: e.tensor_reduce(out=sc[:, 5:6], in_=bc[:, 384:448], axis=AX.X, op=ALU.add),
             rs(BC, 2), [R_scc[5]])
        P.op("dve", lambda e: e.tensor_reduce(out=sc[:, 6:7], in_=bc[:, 448:512], axis=AX.X, op=ALU.add),
             rs(BC, 2), [R_scc[6]])
        act(sc[:, 7:9], sc[:, 5:7], AF.Exp, [R_scc[5], R_scc[6]], [R_scc[7], R_scc[8]])
        tt("dve", sc[:, 9:10], sc[:, 7:8], sc[:, 8:9], ALU.subtract, [R_scc[7], R_scc[8]], [R_scc[9]])
        tsc("dve", sc[:, 1:2], sc[:, 9:10], 0.2, ALU.add, [R_scc[9]], [R_scc[1]])
        tsc("dve", sc[:, 2:3], sc[:, 1:2], -1.0, ALU.mult, [R_scc[1]], [R_scc[2]])
        P.op("dve", lambda e: e.tensor_reduce(out=sc[:, 10:11], in_=bc[:, 256:320], axis=AX.X, op=ALU.max,
                                              apply_absolute_value=True), rs(BC, 2), [R_scc[10]])
        P.op("dve", lambda e: e.tensor_reduce(out=sc[:, 11:12], in_=bc[:, 320:384], axis=AX.X, op=ALU.max,
                                              apply_absolute_value=True), rs(BC, 2), [R_scc[11]])
        tt("dve", sc[:, 12:13], sc[:, 10:11], sc[:, 11:12], ALU.mult, [R_scc[10], R_scc[11]], [R_scc[12]])
        tsc("dve", sc[:, 3:4], sc[:, 12:13], -8.0, ALU.mult, [R_scc[12]], [R_scc[3]])
        P.op("dve", lambda e: e.tensor_reduce(out=sc[:, 13:14], in_=mb[:, 0:256], axis=AX.X, op=ALU.max,
                                              apply_absolute_value=True), rs(MB, 4), [R_scc[13]])
        P.op("dve", lambda e: e.tensor_reduce(out=sc[:, 14:15], in_=mb[:, 256:512], axis=AX.X, op=ALU.max,
                                              apply_absolute_value=True), rs(MB, 4), [R_scc[14]])
        tt("dve", sc[:, 15:16], sc[:, 13:14], sc[:, 14:15], ALU.mult, [R_scc[13], R_scc[14]], [R_scc[15]])
        tsc("dve", sc[:, 4:5], sc[:, 15:16], -16.0, ALU.mult, [R_scc[15]], [R_scc[4]])

        rsd_ctr = [0]
        psA = Ring([0, 1, 2, 3])

        def norm_stage1(xt_ap, xt_res, h_ap, h_res, g_ap, g_res, split=False, junk=None):
            i = rsd_ctr[0] % 32
            rsd_ctr[0] += 1
            c0, c1 = 2 * i, 2 * i + 1
            if junk is None:
                act(h_ap, xt_ap, AF.Square, xt_res, h_res + [R_rsd[c0]], accum=rsd[:, c0:c0 + 1])
            else:
                act(junk[0], xt_ap, AF.Square, xt_res, junk[1] + [R_rsd[c0]], accum=rsd[:, c0:c0 + 1])
            act(rsd[:, c1:c1 + 1], rsd[:, c0:c0 + 1], AF.Ln, [R_rsd[c0], R_scc[0]], [R_rsd[c1]],
                bias=EPSC, scale=1.0 / D)
            act(rsd[:, c0:c0 + 1], rsd[:, c1:c1 + 1], AF.Exp, [R_rsd[c1]], [R_rsd[c0]], scale=-0.5)
            if split:
                act(h_ap, xt_ap, AF.Copy, xt_res + [R_rsd[c0]], h_res, scale=rsd[:, c0:c0 + 1])
                tt("pool", h_ap, h_ap, g_ap, ALU.mult, h_res + [g_res], h_res)
            else:
                stt(h_ap, xt_ap, rsd[:, c0:c0 + 1], g_ap, ALU.mult, ALU.mult,
                    xt_res + [R_rsd[c0], g_res], h_res)

        def norm_stage2(h_ap, h_res, dst3_fn, dst_res_fn):
            for half in range(2):
                b = psA.next()
                for j in range(4):
                    kc = half * 4 + j
                    tr(ps[b][:, j * 128:(j + 1) * 128], h_ap[:, kc * 128:(kc + 1) * 128], identF[:],
                       h_res + [R_ident], [RP[b]])
                if "A" in SKIP:
                    d3 = dst3_fn(half)
                    for j in range(4):
                        tcopy("dve", d3[:, j, :], ps[b][:, j * 128:(j + 1) * 128], [RP[b]], dst_res_fn(half))
                else:
                    tcopy("dve", dst3_fn(half), ps[b][:, :].rearrange("p (k n) -> p k n", n=128), [RP[b]],
                          dst_res_fn(half))

        def norm_transpose(xt_ap, xt_res, h_ap, h_res, g_ap, g_res, dst3_fn, dst_res_fn, split=False, junk=None):
            norm_stage1(xt_ap, xt_res, h_ap, h_res, g_ap, g_res, split, junk)
            norm_stage2(h_ap, h_res, dst3_fn, dst_res_fn)

        MX = [100, 104]
        GM = 108
        MH = [112, 116]
        MN = 120
        WKV = 124
        dma("sp", F4(GM), norm_mem_g.partition_broadcast(128), writes=rs(GM, 4))
        dma("pool", AB[:, WKV * 512:(WKV + 32) * 512].rearrange("p (k n) -> p k n", n=2048),
            w_mem_kv.rearrange("(k p) n -> p k n", p=128), writes=rs(WKV, 32))

        def memnT(kc, a, b):
            return AB[:, MN * 512 + kc * 256 + a:MN * 512 + kc * 256 + b]

        def wkv(kc, a, b):
            return AB[:, WKV * 512 + kc * 2048 + a:WKV * 512 + kc * 2048 + b]

        def do_mem_kv():
            for mt in range(2 if "memkv" not in SKIP else 0):
                dma("sp", F4(MX[mt]), mem[mt * 128:(mt + 1) * 128, :], writes=rs(MX[mt], 4))
                norm_transpose(F4(MX[mt]), rs(MX[mt], 4), F4(MH[mt]), rs(MH[mt], 4), F4(GM), RS[GM],
                               lambda half, mt=mt: AB[:, MN * 512 + half * 1024:MN * 512 + (half + 1) * 1024]
                               .rearrange("p (k n) -> p k n", n=256)[:, :, mt * 128:(mt + 1) * 128],
                               lambda half: rs(MN, 4))
            KT0 = 164
            for hd in range(4 if ("memkv" not in SKIP and "memk" not in SKIP) else 0):
                for dc in range(2):
                    c = hd * 2 + dc
                    b = psA.next()
                    for kc in range(8):
                        mm(ps[b][:, 0:256], wkv(kc, c * 128, (c + 1) * 128), memnT(kc, 0, 256), kc == 0, kc == 7,
                           rs(WKV, 32) + rs(MN, 4), [RP[b]])
                    tcopy("dve", Fv(164 + 2 * dc, 0, 256), ps[b][:, 0:256], [RP[b]], rs(164 + 2 * dc, 2))
                    act(Bv(168 + dc, 0, 256), Fv(164 + 2 * dc, 0, 256), AF.Square, rs(164 + 2 * dc, 2), rs(168 + dc))
                if "K2" in SKIP:
                    continue
                b = psA.next()
                for dc in range(2):
                    mm(ps[b][:, 0:256], ones, Bv(168 + dc, 0, 256), dc == 0, dc == 1, [R_cb] + rs(168 + dc), [RP[b]])
                if "K3" in SKIP:
                    continue
                act(Fv(170, 0, 256), ps[b][:, 0:256], AF.Ln, [RP[b], R_scc[0]], rs(170, 2), bias=EPSC, scale=1.0 / 256)
                act(Fv(172, 0, 256), Fv(170, 0, 256), AF.Exp, rs(170, 2), rs(172, 2), scale=-0.5)
                if "K4" in SKIP:
                    continue
                for dc in range(2):
                    c = hd * 2 + dc
                    stt(kmT[:, c * 256:(c + 1) * 256], Fv(164 + 2 * dc, 0, 256), V_MKG(dc), Fv(172, 0, 256),
                        ALU.mult, ALU.mult, rs(164 + 2 * dc, 2) + rs(172, 2) + [R_vecs], [R_kmT])
            for mt in range(2 if ("memkv" not in SKIP and "memv" not in SKIP) else 0):
                for nh in range(2):
                    b = psA.next()
                    for kc in range(8):
                        mm(ps[b][:, :], memnT(kc, mt * 128, (mt + 1) * 128), wkv(kc, 1024 + nh * 512, 1024 + (nh + 1) * 512),
                           kc == 0, kc == 7, rs(WKV, 32) + rs(MN, 4), [RP[b]])
                    tcopy("dve", vm[:, mt * 1024 + nh * 512:mt * 1024 + (nh + 1) * 512], ps[b][:, :], [RP[b]], [R_vm])


        R_blk = {}
        cast_jobs = []
        for j in range(7):
            R_blk[("in", j)] = Res()
            cast_jobs.append((sc_in[:, j * 1024:(j + 1) * 1024], w_in[:, 3072 + j * 1024:3072 + (j + 1) * 1024],
                              R_blk[("in", j)]))
        for i, wsrc in enumerate([w_attn_o, w_conv_o, w_mem_o, w_o]):
            R_blk[("sq", i)] = Res()
            cast_jobs.append((sc_sq[i], wsrc, R_blk[("sq", i)]))
        for j in range(4):
            R_blk[("w1", j)] = Res()
            cast_jobs.append((sc_w1[:, j * 1024:(j + 1) * 1024], w_mlp_in[:, j * 1024:(j + 1) * 1024], R_blk[("w1", j)]))
        for j in range(4):
            R_blk[("w2", j)] = Res()
            cast_jobs.append((sc_w2[j * 1024:(j + 1) * 1024, :], w_mlp_out[j * 1024:(j + 1) * 1024, :], R_blk[("w2", j)]))
        cast_jobs_small = []
        R_pieces = {}
        for (o_ap, i_ap, r) in cast_jobs:
            R_pieces[id(r)] = [Res() for _ in range(8)]
            for rr in range(8):
                cast_jobs_small.append((o_ap[rr * 128:(rr + 1) * 128, :], i_ap[rr * 128:(rr + 1) * 128, :],
                                        R_pieces[id(r)][rr]))
        cast_jobs = cast_jobs_small
        cast_ptr = [0]
        cast_ops = []

        def issue_casts(n):
            if "casts" in SKIP:
                return
            for _ in range(n):
                if cast_ptr[0] < len(cast_jobs):
                    o_ap, i_ap, r = cast_jobs[cast_ptr[0]]
                    cast_ptr[0] += 1
                    lag = int(os.environ.get("MK_CASTLAG", "4"))
                    dep = cast_ops[-lag] if len(cast_ops) >= lag else None
                    cast_ops.append(dma("pool", o_ap, i_ap, writes=[r], deps=[dep]))

        def hT(kc, a, b):
            return AB[:, kc * 4096 + a:kc * 4096 + b]

        XS = [64, 68, 72, 96]
        HS = [76, 80, 84, 88]
        JUNK = 92
        for t in range(32 if "phaseA" not in SKIP else 0):
            xs_ = XS[t % 4]
            hs_ = HS[t % 4]
            dma("sp", F4(xs_), x[t * 128:(t + 1) * 128, :], writes=rs(xs_, 4))
            norm_transpose(F4(xs_), rs(xs_, 4), F4(hs_), rs(hs_, 4), gmix[:], R_gmix,
                           lambda half, t=t: AB[:, half * 4 * 4096:(half * 4 + 4) * 4096]
                           .rearrange("p (k n) -> p k n", n=4096)[:, :, t * 128:(t + 1) * 128],
                           lambda half, t=t: [RS[kc * 8 + t // 4] for kc in range(half * 4, half * 4 + 4)],
                           split=(t % 2 == 1), junk=(F4(JUNK), rs(JUNK, 4)))
        do_mem_kv()

        QS = [64, 64]
        KS1 = 72
        VS = [96, 96]
        WH = [112, 118]
        PTS = list(range(124, 132))
        SQ = [132, 133, 137]
        OSQ = 134
        OOUT = [135, 136]
        QRAW = [138, 140, 166]
        LNT = [142, 144, 168]
        RSTD = [146, 148, 170]
        VAS = [96, 172]
        ACCS = [150, 152, 154]
        O0S, T1S, ONB = 156, 158, 160
        for VA_ in VAS:
            memset("pool", AB[:, VA_ * 512:VA_ * 512 + 32 * 132].rearrange("p (t c) -> p t c", c=132)[:, :, 128:132],
                   1.0, rs(VA_, 9))

        def wh(par, j, kc):
            base = WH[par] * 512 + j * 1024 + kc * 128
            return AB[:, base:base + 128]

        def load_wh(h):
            par = h % 2
            for j in range(3):
                base = WH[par] * 512 + j * 1024
                dma("pool", AB[:, base:base + 1024].rearrange("p (k n) -> p k n", n=128),
                    w_in[:, j * 1024 + h * 128:j * 1024 + (h + 1) * 128].rearrange("(k p) n -> p k n", p=128),
                    writes=rs(WH[par] + 2 * j, 2))

        if nheads > 0:
            load_wh(0)
        psS = Ring([0, 1, 2, 3])
        pt_ring = Ring(PTS[0::2])
        spair = Ring([0, 2])
        deferred = []

        def run_deferred(n=1000):
            while deferred and n > 0:
                deferred.pop(0)()
                n -= 1
        nrm_i = [0]
        oout_i = [0]
        for h in range(nheads):
            par = h % 2
            pend = []

            def qk_stat_a(info):
                (b, which, c, i) = info
                P.label = "stat w%d c%d" % (which, c)
                b2 = psS.next()
                mm(ps[b2][:, :], blk, Bv(SQ[i]), True, True, [R_cb] + rs(SQ[i]), [RP[b2]])
                act(Fv(LNT[i]), ps[b2][:, :], AF.Ln, [RP[b2], R_scc[0]], rs(LNT[i], 2), bias=EPSC, scale=1.0 / 64)
                act(Fv(RSTD[i]), Fv(LNT[i]), AF.Exp, rs(LNT[i], 2), rs(RSTD[i], 2), scale=-0.5)

            def qk_stat_b(info):
                (b, which, c, i) = info
                if which == 0:
                    dst = QS[par] + c
                    stt(Bv(dst), Fv(QRAW[i]), V_GQ8, Fv(RSTD[i]), ALU.mult, ALU.mult,
                        rs(QRAW[i], 2) + rs(RSTD[i], 2) + [R_vecs], rs(dst))
                else:
                    dst = KS1 + c
                    stt(Bv(dst), Fv(QRAW[i]), V_GK, Fv(RSTD[i]), ALU.mult, ALU.mult,
                        rs(QRAW[i], 2) + rs(RSTD[i], 2) + [R_vecs], rs(dst))

            for which in range(2):
                for c in range(8):
                    i = nrm_i[0] % 3
                    nrm_i[0] += 1
                    old_info = pend.pop(0) if len(pend) >= 2 else None
                    if old_info is not None:
                        qk_stat_a(old_info)
                    b = psS.next()
                    P.label = "proj h%d w%d c%d" % (h, which, c)
                    for kc in range(8):
                        mm(ps[b][:, :], wh(par, which, kc), hT(kc, c * 512, (c + 1) * 512), kc == 0, kc == 7,
                           rs(WH[par], 6) + rs(kc * 8 + c), [RP[b]])
                    tcopy("dve", Fv(QRAW[i]), ps[b][:, :], [RP[b]], rs(QRAW[i], 2))
                    tt("pool", Bv(SQ[i]), Fv(QRAW[i]), Fv(QRAW[i]), ALU.mult, rs(QRAW[i], 2), rs(SQ[i]))
                    if old_info is not None:
                        qk_stat_b(old_info)
                    pend.append((b, which, c, i))
                    if c in (0, 2, 4, 6):
                        run_deferred(1)
            while pend:
                info = pend.pop(0)
                qk_stat_a(info)
                qk_stat_b(info)
            VA = VAS[par]

            def vproj_unit(hh, g, j, b):
                pp = hh % 2
                t = g * 4 + j
                for kc in range(8):
                    mm(ps[b][:, j * 128:(j + 1) * 128], hT(kc, t * 128, (t + 1) * 128), wh(pp, 2, kc),
                       kc == 0, kc == 7, rs(WH[pp], 6) + rs(kc * 8 + g), [RP[b]])
                if j == 3:
                    va = VAS[pp]
                    tcopy("dve", AB[:, va * 512 + g * 528:va * 512 + (g + 1) * 528].rearrange(
                        "p (t c) -> p t c", c=132)[:, :, 0:128],
                        ps[b][:, :].rearrange("p (t c) -> p t c", c=128), [RP[b]], rs(va, 9))

            def vproj_half(hh, g, j, hf, b):
                pp = hh % 2
                t = g * 4 + j
                for kc in range(hf * 4, hf * 4 + 4):
                    mm(ps[b][:, j * 128:(j + 1) * 128], hT(kc, t * 128, (t + 1) * 128), wh(pp, 2, kc),
                       kc == 0, kc == 7, rs(WH[pp], 6) + rs(kc * 8 + g), [RP[b]])
                if j == 3 and hf == 1:
                    va = VAS[pp]
                    tcopy("dve", AB[:, va * 512 + g * 528:va * 512 + (g + 1) * 528].rearrange(
                        "p (t c) -> p t c", c=132)[:, :, 0:128],
                        ps[b][:, :].rearrange("p (t c) -> p t c", c=128), [RP[b]], rs(va, 9))

            if h == 0:
                for g in range(8):
                    b = psS.next()
                    for j in range(4):
                        vproj_unit(0, g, j, b)
            if h + 1 < nheads:
                load_wh(h + 1)
                vunits = [(g, j, hf) for g in range(8) for j in range(4) for hf in range(2)]
            else:
                vunits = []
            ktctr = [0]
            def qk(qc, kt):
                qslot = QS[par] + qc
                j = kt - 4 * qc
                off = max(0, j) * 128
                ko = (kt % 4) * 128
                sb_ = []
                pair = spair.next()
                kslot = KS1 + kt // 4
                for sub in range(2):
                    b = pair + sub
                    p0, p1 = sub * 64, (sub + 1) * 64
                    mm(ps[b][:, off:512], AB[p0:p1, kslot * 512 + ko:kslot * 512 + ko + 128],
                       AB[p0:p1, qslot * 512 + off:qslot * 512 + 512], True, j < 0,
                       rs(kslot) + rs(qslot), [RP[b]])
                    sb_.append(b)
                if j >= 0:
                    for sub in range(2):
                        b = pair + sub
                        mm(ps[b][:, off:off + 128], identB, negtri, False, True, [R_cb], [RP[b]])
                return sb_

            def softmax(qc, kt, sb_):
                j = kt - 4 * qc
                off = max(0, j) * 128
                p0 = pt_ring.next()
                pts = [p0, p0 + 1]
                b0 = sb_[0]
                in3 = psall[:, b0 * 512:(b0 + 2) * 512].rearrange("p (s n) -> p s n", n=512)[:, :, off:512]
                out3 = AB[:, p0 * 512:(p0 + 2) * 512].rearrange("p (s n) -> p s n", n=512)[:, :, off:512]
                act(out3, in3, AF.Exp, [RP[b0], RP[b0 + 1], R_scc[3]], rs(p0, 2), bias=NEGM, scale=1.0)
                return pts

            def pv(qc, kt, pts):
                nkt = 4 * qc + 4
                j = kt - 4 * qc
                q0 = max(0, j)
                vcol = VA * 512 + kt * 132
                for sub in range(2):
                    for qt in range(q0, 4):
                        a_ = sub * 4 + qt
                        bank = 4 + a_ // 3
                        col = (a_ % 3) * 129
                        mm(ps[bank][:, col:col + 129], Bv(pts[sub], qt * 128, (qt + 1) * 128),
                           AB[:, vcol:vcol + 129], kt == 0 and a_ % 3 == 0, kt == nkt - 1,
                           rs(VA, 9) + rs(pts[sub]), [RP[bank]], sgc=True)

            steps = [(qc, kt) for qc in range(nqc) for kt in range(4 * qc + 4)]
            cur_pts = softmax(*steps[0], qk(*steps[0])) if steps else None
            for si, (qc, kt) in enumerate(steps):
                nkt = 4 * qc + 4
                if kt == 0:
                    issue_casts(3)
                nxt = steps[si + 1] if si + 1 < len(steps) else None
                sb_n = qk(*nxt) if nxt is not None else None
                pv(qc, kt, cur_pts)
                last = kt == nkt - 1
                if last:
                    epilogue_now = True
                else:
                    epilogue_now = False
                if epilogue_now:
                    pass
                if not last:
                    if nxt is not None:
                        cur_pts = softmax(*nxt, sb_n)
                    if kt in (0, 2, 4, 6, 8):
                        run_deferred(1)
                    ktctr[0] += 1
                    if vunits and ktctr[0] % 2 == 1:
                        g_, j_, half_ = vunits.pop(0)
                        vproj_half(h + 1, g_, j_, half_, 7)
                    continue
                run_deferred()
                for bk in range(3):
                    n_ = 387 if bk < 2 else 258
                    tcopy("dve", Fv(ACCS[bk], 0, n_), ps[4 + bk][:, 0:n_], [RP[4 + bk]], rs(ACCS[bk], 2))

                def accv(a_, c0, c1):
                    return Fv(ACCS[a_ // 3], (a_ % 3) * 129 + c0, (a_ % 3) * 129 + c1)

                def epA():
                    for bk in range(3):
                        n3 = 3 if bk < 2 else 2
                        src = Fv(ACCS[bk], 0, n3 * 129).rearrange("p (a c) -> p a c", c=129)[:, :, 128:129]
                        P.op("dve", lambda e, o_=epi[:, 3 * bk:3 * bk + n3], i_=src: e.reciprocal(out=o_, in_=i_),
                             rs(ACCS[bk], 2), [R_epi[0]])
                    for qt in range(4):
                        tsc("dve", Fv(O0S, qt * 128, (qt + 1) * 128), accv(qt, 0, 128), epi[:, qt:qt + 1], ALU.mult,
                            rs(ACCS[qt // 3], 2) + [R_epi[0]], rs(O0S, 2))
                        tsc("dve", Fv(T1S, qt * 128, (qt + 1) * 128), accv(4 + qt, 0, 128), epi[:, 4 + qt:5 + qt],
                            ALU.mult, rs(ACCS[(4 + qt) // 3], 2) + [R_epi[0]], rs(T1S, 2))

                def epB():
                    stt(Fv(O0S), Fv(T1S), NEGLAM, Fv(O0S), ALU.mult, ALU.add,
                        rs(T1S, 2) + rs(O0S, 2) + [R_scc[2]], rs(O0S, 2))
                    tt("dve", Fv(T1S), Fv(O0S), Fv(O0S), ALU.mult, rs(O0S, 2), rs(T1S, 2))
                    P.op("dve", lambda e: e.tensor_reduce(out=epi[:, 8:12],
                                                          in_=Fv(T1S).rearrange("p (a c) -> p a c", c=128),
                                                          axis=AX.X, op=ALU.add), rs(T1S, 2), [R_epi[1]])

                def epC():
                    act(epi[:, 12:16], epi[:, 8:12], AF.Ln, [R_epi[1], R_scc[0]], [R_epi[2]], bias=EPSC, scale=1.0 / 128)
                    act(epi[:, 16:20], epi[:, 12:16], AF.Exp, [R_epi[2]], [R_epi[3]], scale=-0.5)
                    for qt in range(4):
                        stt(Fv(ONB, qt * 128, (qt + 1) * 128), Fv(O0S, qt * 128, (qt + 1) * 128),
                            epi[:, 16 + qt:17 + qt], gsubb[:], ALU.mult, ALU.mult,
                            rs(O0S, 2) + [R_epi[3], R_gsubb], rs(ONB, 2))

                R_o = Res()
                R_blk[("oT", h, qc)] = R_o

                def epD(h=h, qc=qc, R_o=R_o):
                    b2 = psS.next()
                    for qt in range(4):
                        tr(ps[b2][:, qt * 128:(qt + 1) * 128], Fv(ONB, qt * 128, (qt + 1) * 128), identF[:],
                           rs(ONB, 2) + [R_ident], [RP[b2]])
                    oo = OOUT[oout_i[0] % 2]
                    oout_i[0] += 1
                    tcopy("dve", Bv(oo), ps[b2][:, :], [RP[b2]], rs(oo))
                    dma("pool", oTd[h * 128:(h + 1) * 128, qc * 512:(qc + 1) * 512], Bv(oo), reads=rs(oo), writes=[R_o])

                deferred.extend([epA, epB, epC, epD])
                if nxt is not None:
                    cur_pts = softmax(*nxt, sb_n)
            while vunits:
                g_, j_, half_ = vunits.pop(0)
                vproj_half(h + 1, g_, j_, half_, 7)
        run_deferred()
        issue_casts(len(cast_jobs))

        XIN = [[0 + 16 * s + 4 * t for t in range(4)] for s in range(2)]
        HF = [32, 36]
        HT = 40
        OTC = [48, 56]
        CONVT = 64
        OMT = 72
        QMT = 80
        MRGT = 88
        AT = 64
        MRG = 96
        WR = [112, 120, 128, 136]
        TMP = 144
        XCS = [TMP + 2 * i for i in range(4)]
        INN = [TMP + 8 + 3 * i for i in range(4)]
        YT = [TMP + 20, TMP + 22]
        GT = [TMP + 24 + 2 * i for i in range(4)]
        MT = [TMP + 32 + 2 * i for i in range(5)]

        def inn(i, a, b):
            return AB[:, INN[i] * 512:(INN[i] + 3) * 512].bitcast(F32)[:, a:b]

        groups = []
        for T in range(nchunks):
            gl = []
            for j in range(2):
                for kind, cb0 in (("xc", 0), ("gc", 2048), ("gb", 1024)):
                    gl.append((kind, j, sc_in, 0, cb0 + j * 512, R_blk[("in", (cb0 + j * 512) // 1024)]))
            def gate_bo(br, j):
                gl.append(("gate", (br, j), sc_in, 0, 4096 + br * 1024 + j * 512, R_blk[("in", 4 + br)]))
                gl.append(("bo", (br, j), sc_sq[br], 0, j * 512, R_blk[("sq", br)]))
            gl.append(("qm", 0, sc_in, 0, 3072, R_blk[("in", 3)]))
            gate_bo(0, 0)
            gl.append(("qm", 1, sc_in, 0, 3072 + 512, R_blk[("in", 3)]))
            gate_bo(0, 1)
            for br in (1, 2):
                for j in range(2):
                    gate_bo(br, j)
            for j in range(2):
                gl.append(("wo", j, sc_sq[3], 0, j * 512, R_blk[("sq", 3)]))
            for j in range(8):
                gl.append(("w1", j, sc_w1, 0, j * 512, R_blk[("w1", j // 2)]))
            for nh in range(2):
                for fb in range(4):
                    gl.append(("w2", (nh, fb), sc_w2, fb * 1024, nh * 512, R_blk[("w2", fb)]))
            groups.extend(gl)
        gptr = [0]
        gissued = [0]

        def issue_groups(upto):
            while gissued[0] < min(upto, len(groups)):
                g = gissued[0]
                kind, idx, scr, r0, c0, rb = groups[g]
                slot = WR[g % 4]
                dma("sp", AB[:, slot * 512:(slot + 8) * 512].rearrange("p (k n) -> p k n", n=512),
                    scr[r0:r0 + 1024, c0:c0 + 512].rearrange("(k p) n -> p k n", p=128),
                    reads=R_pieces[id(rb)], writes=rs(slot, 8))
                gissued[0] += 1

        def next_group(kind_expect, hold=0):
            g = gptr[0]
            gptr[0] += 1
            issue_groups(g + 4 - hold)
            kind, idx, scr, r0, c0, rb = groups[g]
            assert kind == kind_expect, (kind, kind_expect)
            slot = WR[g % 4]

            def wv(kc, a, b):
                return AB[:, slot * 512 + kc * 512 + a:slot * 512 + kc * 512 + b]
            return idx, wv, rs(slot, 8)

        def load_chunk_inputs(T):
            s = T % 2
            for t in range(4):
                dma("sp", F4(XIN[s][t]), x[T * 512 + t * 128:T * 512 + (t + 1) * 128, :], writes=rs(XIN[s][t], 4))
            dma("sp", AB[:, OTC[s] * 512:(OTC[s] + 8) * 512].rearrange("p (k n) -> p k n", n=512),
                oTd[:, T * 512:(T + 1) * 512].rearrange("(k p) n -> p k n", p=128),
                reads=[R_blk[("oT", hh, T)] for hh in range(nheads) if ("oT", hh, T) in R_blk],
                writes=rs(OTC[s], 8))

        psC = Ring([0, 1, 2, 3, 4, 5, 6, 7])
        hf_i = [0]
        if nchunks > 0:
            load_chunk_inputs(0)
        for T in range(nchunks):
            s = T % 2
            if T + 1 < nchunks:
                load_chunk_inputs(T + 1)
            issue_groups(gptr[0] + 3)
            HFB = [32, 36, 144, 148]

            def ht_dst3(half, t):
                return AB[:, (HT + half * 4) * 512:(HT + half * 4 + 4) * 512].rearrange(
                    "p (k n) -> p k n", n=512)[:, :, t * 128:(t + 1) * 128]

            if T == 0 or "D" in SKIP:
                for t in range(4):
                    norm_transpose(F4(XIN[s][t]), rs(XIN[s][t], 4), F4(HFB[t]), rs(HFB[t], 4), gmix[:], R_gmix,
                                   lambda half, t=t: ht_dst3(half, t), lambda half: rs(HT + half * 4, 4))

            def proj(wv, wres, j, srcslot):
                b = psC.next()
                for kc in range(8):
                    mm(ps[b][:, :], wv(kc, j * 128, (j + 1) * 128), Bv(srcslot + kc), kc == 0, kc == 7,
                       wres + rs(srcslot + kc), [RP[b]])
                return b

            for half in range(2):
                idx, wv, wres = next_group("xc")
                for j in range(4):
                    b = proj(wv, wres, j, HT)
                    tcopy("dve", Fv(XCS[j]), ps[b][:, :], [RP[b]], rs(XCS[j], 2))
                idx, wv, wres = next_group("gc")
                for j in range(4):
                    c = half * 4 + j
                    b = proj(wv, wres, j, HT)
                    tcopy("dve", inn(j, 0, 2), halo[:, 2 * c:2 * c + 2], [R_halo[c]], rs(INN[j], 3))
                    tt("dve", inn(j, 2, 514), ps[b][:, :], Fv(XCS[j]), ALU.mult, [RP[b]] + rs(XCS[j], 2), rs(INN[j], 3))
                    tcopy("dve", halo[:, 2 * c:2 * c + 2], inn(j, 512, 514), rs(INN[j], 3), [R_halo[c]])
                idx, wv, wres = next_group("gb")
                for j in range(4):
                    c = half * 4 + j
                    y = YT[j % 2]
                    tsc("dve", Fv(y), inn(j, 0, 512), V_CW(0, c), ALU.mult, rs(INN[j], 3) + [R_vecs], rs(y, 2))
                    stt(Fv(y), inn(j, 1, 513), V_CW(1, c), Fv(y), ALU.mult, ALU.add, rs(INN[j], 3) + rs(y, 2) + [R_vecs], rs(y, 2))
                    stt(Fv(y), inn(j, 2, 514), V_CW(2, c), Fv(y), ALU.mult, ALU.add, rs(INN[j], 3) + rs(y, 2) + [R_vecs], rs(y, 2))
                    b = proj(wv, wres, j, HT)
                    tt("dve", Bv(CONVT + c), ps[b][:, :], Fv(y), ALU.mult, [RP[b]] + rs(y, 2), rs(CONVT + c))
            QRAW_M = [[144, 146], [148, 150]]
            SQ_M = [[164, 165], [166, 167]]
            LN_M = [152, 154]
            RS_M = [156, 158]
            PTM = [[160, 161], [162, 163]]
            RCP = [176, 178]

            def mS1(hd, wv, wres, hl):
                p = hd % 2
                for dc in range(2):
                    b = proj(wv, wres, hl * 2 + dc, HT)
                    tcopy("dve", Fv(QRAW_M[p][dc]), ps[b][:, :], [RP[b]], rs(QRAW_M[p][dc], 2))
                    tt("pool", Bv(SQ_M[p][dc]), Fv(QRAW_M[p][dc]), Fv(QRAW_M[p][dc]), ALU.mult,
                       rs(QRAW_M[p][dc], 2), rs(SQ_M[p][dc]))

            def mS2(hd):
                p = hd % 2
                b = psC.next()
                for dc in range(2):
                    mm(ps[b][:, :], ones, Bv(SQ_M[p][dc]), dc == 0, dc == 1, [R_cb] + rs(SQ_M[p][dc]), [RP[b]])
                act(Fv(LN_M[p]), ps[b][:, :], AF.Ln, [RP[b], R_scc[0]], rs(LN_M[p], 2), bias=EPSC, scale=1.0 / 256)
                act(Fv(RS_M[p]), Fv(LN_M[p]), AF.Exp, rs(LN_M[p], 2), rs(RS_M[p], 2), scale=-0.5)
                for dc in range(2):
                    stt(Bv(QMT + hd * 2 + dc), Fv(QRAW_M[p][dc]), V_MQG16(dc), Fv(RS_M[p]), ALU.mult, ALU.mult,
                        rs(QRAW_M[p][dc], 2) + rs(RS_M[p], 2) + [R_vecs], rs(QMT + hd * 2 + dc))

            def mS3(hd):
                p = hd % 2
                for mt in range(2):
                    b = psC.next()
                    for dc in range(2):
                        c = hd * 2 + dc
                        mm(ps[b][:, :], kmT[:, c * 256 + mt * 128:c * 256 + (mt + 1) * 128], Bv(QMT + c),
                           dc == 0, dc == 1, [R_kmT] + rs(QMT + c), [RP[b]])
                    act(Bv(PTM[p][mt]), ps[b][:, :], AF.Exp, [RP[b], R_scc[4]], rs(PTM[p][mt]), bias=NEGMM, scale=1.0)
                bl = psC.next()
                for mt in range(2):
                    mm(ps[bl][:, :], ones, Bv(PTM[p][mt]), mt == 0, mt == 1, [R_cb] + rs(PTM[p][mt]), [RP[bl]])
                tcopy("dve", Fv(RCP[p]), ps[bl][:, :], [RP[bl]], rs(RCP[p], 2))
                P.op("dve", lambda e, a=Fv(RCP[p]): e.reciprocal(out=a, in_=a), rs(RCP[p], 2), rs(RCP[p], 2))

            def mS4(hd):
                p = hd % 2
                for ec in range(2):
                    b = psC.next()
                    for mt in range(2):
                        col = mt * 1024 + hd * 256 + ec * 128
                        mm(ps[b][:, :], vm[:, col:col + 128], Bv(PTM[p][mt]), mt == 0, mt == 1,
                           [R_vm] + rs(PTM[p][mt]), [RP[b]])
                    tt("dve", Bv(OMT + hd * 2 + ec), ps[b][:, :], Fv(RCP[p]), ALU.mult,
                       [RP[b]] + rs(RCP[p], 2), rs(OMT + hd * 2 + ec))

            srcs = [OTC[s], CONVT, OMT]

            def gate_block(br, half):
                idx, wv, wres = next_group("gate")
                for j in range(4):
                    n = br * 8 + half * 4 + j
                    b = proj(wv, wres, j, HT)
                    act(Fv(GT[j]), ps[b][:, :], AF.Sigmoid, [RP[b], R_vecs], rs(GT[j], 2), bias=V_BG(n), scale=1.0)
                idx, wv, wres = next_group("bo")
                for j in range(4):
                    f = half * 4 + j
                    b = proj(wv, wres, j, srcs[br])
                    if br == 0:
                        tt("dve", Fv(MRG + 2 * f), ps[b][:, :], Fv(GT[j]), ALU.mult,
                           [RP[b]] + rs(GT[j], 2), rs(MRG + 2 * f, 2))
                    else:
                        tt("dve", Fv(GT[j]), ps[b][:, :], Fv(GT[j]), ALU.mult, [RP[b]] + rs(GT[j], 2), rs(GT[j], 2))
                        if br == 1:
                            tt("pool", Fv(MRG + 2 * f), Fv(MRG + 2 * f), Fv(GT[j]), ALU.add,
                               rs(MRG + 2 * f, 2) + rs(GT[j], 2), rs(MRG + 2 * f, 2))
                        else:
                            tt("pool", Bv(MRGT + f), Fv(MRG + 2 * f), Fv(GT[j]), ALU.add,
                               rs(MRG + 2 * f, 2) + rs(GT[j], 2), rs(MRGT + f))

            if "C" in SKIP:
                idx, wv, wres = next_group("qm")
                for hd in (0, 1):
                    mS1(hd, wv, wres, hd)
                    mS2(hd)
                    mS3(hd)
                    mS4(hd)
                gate_block(0, 0)
                idx, wv, wres = next_group("qm")
                for hd in (2, 3):
                    mS1(hd, wv, wres, hd - 2)
                    mS2(hd)
                    mS3(hd)
                    mS4(hd)
                gate_block(0, 1)
                gate_block(1, 0)
                gate_block(1, 1)
            else:
                idx, wv, wres = next_group("qm")
                mS1(0, wv, wres, 0)
                mS1(1, wv, wres, 1)
                gate_block(0, 0)
                mS2(0)
                mS2(1)
                idx, wv, wres = next_group("qm")
                mS1(2, wv, wres, 0)
                mS1(3, wv, wres, 1)
                gate_block(0, 1)
                mS3(0)
                mS3(1)
                mS2(2)
                mS2(3)
                gate_block(1, 0)
                mS4(0)
                mS4(1)
                mS3(2)
                mS3(3)
                gate_block(1, 1)
                mS4(2)
                mS4(3)
            gate_block(2, 0)
            gate_block(2, 1)
            wo = [next_group("wo"), next_group("wo", hold=1)]

            def h2_stage2(t):
                norm_stage2(F4(HFB[t]), rs(HFB[t], 4), lambda half, t=t: ht_dst3(half, t),
                            lambda half: rs(HT + half * 4, 4))

            for t in range(4):
                for nh in range(2):
                    idx, wv, wres = wo[nh]
                    b = psC.next()
                    for kc in range(8):
                        mm(ps[b][:, :], Bv(MRGT + kc, t * 128, (t + 1) * 128), wv(kc, 0, 512), kc == 0, kc == 7,
                           wres + rs(MRGT + kc), [RP[b]])
                    xt = F4(XIN[s][t], nh * 512, (nh + 1) * 512)
                    tt("dve", xt, ps[b][:, :], xt, ALU.add, [RP[b]] + rs(XIN[s][t], 4), rs(XIN[s][t], 4))
                norm_stage1(F4(XIN[s][t]), rs(XIN[s][t], 4), F4(HFB[t]), rs(HFB[t], 4), gmlp[:], R_gmlp)
                if t >= 2:
                    h2_stage2(t - 2)
            h2_stage2(2)
            h2_stage2(3)
            for g8 in range(8):
                idx, wv, wres = next_group("w1")
                for j in range(4):
                    fc = g8 * 4 + j
                    b = proj(wv, wres, j, HT)
                    r_ = GT[j]
                    act(Fv(r_), ps[b][:, :], AF.Relu, [RP[b]], rs(r_, 2))
                    tt("pool", Bv(AT + fc), Fv(r_), Fv(r_), ALU.mult, rs(r_, 2), rs(AT + fc))
            pre = (T + 1 < nchunks) and ("D" not in SKIP)
            s2 = (T + 1) % 2
            if pre:
                for t in range(4):
                    norm_stage1(F4(XIN[s2][t]), rs(XIN[s2][t], 4), F4(HFB[t]), rs(HFB[t], 4), gmix[:], R_gmix)
            for nh in range(2):
                accb = [psC.next() for _ in range(4)]
                for fb in range(4):
                    idx, wv, wres = next_group("w2")
                    for t in range(4):
                        for kc in range(8):
                            fc = fb * 8 + kc
                            mm(ps[accb[t]][:, :], Bv(AT + fc, t * 128, (t + 1) * 128), wv(kc, 0, 512),
                               fb == 0 and kc == 0, fb == 3 and kc == 7, wres + rs(AT + fc), [RP[accb[t]]])
                    if pre and nh == 1:
                        norm_stage2(F4(HFB[fb]), rs(HFB[fb], 4), lambda half, t=fb: ht_dst3(half, t),
                                    lambda half: rs(HT + half * 4, 4))
                for t in range(4):
                    xt = F4(XIN[s][t], nh * 512, (nh + 1) * 512)
                    tt("dve", xt, ps[accb[t]][:, :], xt, ALU.add, [RP[accb[t]]] + rs(XIN[s][t], 4), rs(XIN[s][t], 4))
                    if nh == 1:
                        out_dmas.append(dma("sp", out[T * 512 + t * 128:T * 512 + (t + 1) * 128, :], F4(XIN[s][t]),
                                            reads=rs(XIN[s][t], 4)))

        tail = [o for q in P.dlast for o in P.dlast[q] if o is not None]
        P.op("sp", None, deps=tail + out_dmas)
        P.finalize()
        build.stats = P.stats()
        build.P = P
        with nc.Block() as block:
            P.emit(block)
    return nc


def make_consts():
    c = np.zeros((128, 768), np.float32)
    c[:, 0:128] = np.eye(128, dtype=np.float32)
    k = np.arange(128)[:, None]
    q = np.arange(128)[None, :]
    c[:, 128:256] = (q >= k).astype(np.float32)
    c[:, 256:384] = ((k // 64) == (q // 64)).astype(np.float32)
    c[:, 384:512] = 1.0
    c[:, 512:640] = np.eye(128, dtype=np.float32)
    c[:, 640:768] = np.where(k > q, -30000.0, 0.0).astype(np.float32)
    return c


_NC_CACHE = {}


def kernel(**inputs):
    n = 8
    if "nc" not in _NC_CACHE:
        _NC_CACHE["nc"] = build()
    nc = _NC_CACHE["nc"]
    consts = make_consts()
    shared = {"consts": consts}
    for k, v in inputs.items():
        if k in ("x", "mem"):
            continue
        a = np.ascontiguousarray(np.asarray(v, dtype=np.float32))
        shared[k] = a.reshape(a.shape[1:])
    x = np.asarray(inputs["x"], dtype=np.float32)
    mem = np.asarray(inputs["mem"], dtype=np.float32)
    in_maps = []
    for b in range(n):
        m = dict(shared)
        m["x"] = np.ascontiguousarray(x[b])
        m["mem"] = np.ascontiguousarray(mem[b])
        in_maps.append(m)
    res = run_bass_kernel_spmd(nc, in_maps, core_ids=list(range(n)))
    return np.stack([np.asarray(r["out"], dtype=np.float32) for r in res.results], axis=0)
```

```python
import numpy as np
from contextlib import ExitStack
import concourse.bass as bass
import concourse.mybir as mybir
from concourse.bass_utils import run_bass_kernel_spmd

F32 = mybir.dt.float32
BF16 = mybir.dt.bfloat16
AF = mybir.ActivationFunctionType
ALU = mybir.AluOpType
AX = mybir.AxisListType

ENGS = ["pe", "act", "dve", "pool", "sp"]
BLOCKNAME = {"pe": "tensor", "act": "scalar", "dve": "vector", "pool": "gpsimd", "sp": "sync"}
N_DMA_SEMS = {"sp": 32, "pool": 16}

S = 4096
D = 1024
NSLOT = 186
EPS = 1e-6


class Res:
    __slots__ = ("name", "w", "r")

    def __init__(self, name=""):
        self.name = name
        self.w = None
        self.r = []


class Op:
    __slots__ = ("eng", "fn", "deps", "pos", "is_dma", "dsem", "dval", "mark", "markidx",
                 "clock", "waits", "label")


class Prog:
    def __init__(self, nc, es):
        self.nc = nc
        self.ops = {e: [] for e in ENGS}
        self.all = []
        self.esem = {e: es.enter_context(nc.semaphore("s_" + e)) for e in ENGS}
        self.dsems = {q: [es.enter_context(nc.semaphore("d_%s_%d" % (q, i))) for i in range(n)]
                      for q, n in N_DMA_SEMS.items()}
        self.dcnt = {q: [0] * n for q, n in N_DMA_SEMS.items()}
        self.dlast = {q: [None] * n for q, n in N_DMA_SEMS.items()}
        self.drr = {q: 0 for q in N_DMA_SEMS}
        self.label = ""

    def op(self, eng, fn, reads=(), writes=(), deps=(), dma=False):
        o = Op()
        o.eng = eng
        o.fn = fn
        o.is_dma = dma
        o.mark = False
        o.markidx = None
        o.dsem = None
        o.dval = None
        o.label = self.label
        d = {}
        for x in deps:
            if x is not None:
                d[x] = True
        for r in reads:
            if r.w is not None:
                d[r.w] = True
        for w in writes:
            if w.w is not None:
                d[w.w] = True
            for x in w.r:
                if x not in d:
                    d[x] = False
        for r in reads:
            r.r.append(o)
        for w in writes:
            w.w = o
            w.r = []
        if dma:
            q = eng
            i = self.drr[q]
            self.drr[q] = (i + 1) % len(self.dsems[q])
            prev = self.dlast[q][i]
            if prev is not None:
                d[prev] = True
            self.dcnt[q][i] += 1
            o.dsem = self.dsems[q][i]
            o.dval = 16 * self.dcnt[q][i]
            self.dlast[q][i] = o
        d.pop(o, None)
        o.deps = d
        o.pos = len(self.ops[eng])
        self.ops[eng].append(o)
        self.all.append(o)
        return o

    def finalize(self):
        ck = {e: {f: -1 for f in ENGS} for e in ENGS}
        kd = {e: set() for e in ENGS}
        for o in self.all:
            e = o.eng
            c = ck[e]
            best = {}
            dwaits = []
            for d, strong in o.deps.items():
                if d.is_dma:
                    if d not in kd[e]:
                        kd[e].add(d)
                        dwaits.append(d)
                    continue
                if d.eng == e:
                    if e in ("pe", "sp"):
                        continue
                if c[d.eng] >= d.pos:
                    continue
                b = best.get(d.eng)
                if b is None or d.pos > b.pos:
                    best[d.eng] = d
            waits = []
            for d in dwaits:
                for f in ENGS:
                    if d.clock[f] > c[f]:
                        c[f] = d.clock[f]
                waits.append(d)
            for f, d in best.items():
                if c[f] >= d.pos:
                    continue
                d.mark = True
                for g in ENGS:
                    if d.clock[g] > c[g]:
                        c[g] = d.clock[g]
                if d.pos > c[f]:
                    c[f] = d.pos
                waits.append(d)
            o.waits = waits
            o.clock = dict(c)
        self.nmarks = {}
        for e in ENGS:
            n = 0
            for o in self.ops[e]:
                if o.mark:
                    n += 1
                    o.markidx = n
            self.nmarks[e] = n

    def emit(self, block):
        for e in ENGS:
            if not self.ops[e]:
                continue

            def body(eng, e=e):
                for o in self.ops[e]:
                    for d in o.waits:
                        if d.is_dma:
                            eng.wait_ge(d.dsem, d.dval)
                        else:
                            eng.wait_ge(self.esem[d.eng], d.markidx)
                    if o.fn is None:
                        continue
                    ins = o.fn(eng)
                    if o.is_dma:
                        ins.then_inc(o.dsem, 16)
                    elif o.mark:
                        ins.then_inc(self.esem[e], 1)

            getattr(block, BLOCKNAME[e])(body)

    def stats(self):
        return {e: (len(self.ops[e]), self.nmarks.get(e)) for e in ENGS}


import os
SKIP = set(os.environ.get("MK_SKIP", "").split(","))


def build(nheads=8, nqc=8, nchunks=8, dbg=False):
    nc = bass.Bass("TRN2", target_bir_lowering=False)

    def din(name, shape):
        return nc.dram_tensor(name, list(shape), F32, kind="ExternalInput").ap()

    x = din("x", [S, D])
    mem = din("mem", [256, D])
    norm_mix_g = din("norm_mix_g", [D])
    norm_mem_g = din("norm_mem_g", [D])
    w_in = din("w_in", [D, 10240])
    b_gate = din("b_gate", [3072])
    q_norm_g = din("q_norm_g", [64])
    k_norm_g = din("k_norm_g", [64])
    lam_q1 = din("lam_q1", [64])
    lam_k1 = din("lam_k1", [64])
    lam_q2 = din("lam_q2", [64])
    lam_k2 = din("lam_k2", [64])
    subln_g = din("subln_g", [128])
    w_attn_o = din("w_attn_o", [D, D])
    conv_w = din("conv_w", [3, D])
    w_conv_o = din("w_conv_o", [D, D])
    w_mem_kv = din("w_mem_kv", [D, 2048])
    mq_norm_g = din("mq_norm_g", [256])
    mk_norm_g = din("mk_norm_g", [256])
    w_mem_o = din("w_mem_o", [D, D])
    w_o = din("w_o", [D, D])
    norm_mlp_g = din("norm_mlp_g", [D])
    w_mlp_in = din("w_mlp_in", [D, 4096])
    w_mlp_out = din("w_mlp_out", [4096, D])
    consts = din("consts", [128, 768])
    out = nc.dram_tensor("out", [S, D], F32, kind="ExternalOutput").ap()

    okind = "ExternalOutput" if dbg else "Internal"
    sc_in = nc.dram_tensor("sc_in", [D, 7168], BF16, kind="Internal").ap()
    sc_sq = [nc.dram_tensor("sc_sq%d" % i, [D, D], BF16, kind="Internal").ap() for i in range(4)]
    sc_w1 = nc.dram_tensor("sc_w1", [D, 4096], BF16, kind="Internal").ap()
    sc_w2 = nc.dram_tensor("sc_w2", [4096, D], BF16, kind="Internal").ap()
    oTd = nc.dram_tensor("oTd", [D, S], BF16, kind=okind).ap()

    es = ExitStack()
    with es:
        P = Prog(nc, es)

        def sbt(name, shape, dt):
            return es.enter_context(nc.sbuf_tensor(name, shape, dt))

        AB = sbt("arena", [128, NSLOT * 512], BF16)
        RS = [Res("s%d" % i) for i in range(NSLOT)]
        identF = sbt("identF", [128, 128], F32)
        cb = sbt("cb", [128, 640], BF16)
        vecs = sbt("vecs", [128, 64], F32)
        sc = sbt("sc", [128, 64], F32)
        gmix = sbt("gmix", [128, D], F32)
        gmlp = sbt("gmlp", [128, D], F32)
        kmT = sbt("kmT", [128, 8 * 256], BF16)
        vm = sbt("vm", [128, 2 * D], BF16)
        halo = sbt("halo", [128, 16], F32)
        rsd = sbt("rsd", [128, 64], F32)
        epi = sbt("epi", [128, 32], F32)
        gsubb = sbt("gsubb", [128, 128], F32)
        R_epi = [Res("epi%d" % i) for i in range(4)]
        R_gsubb = Res("gsubb")
        R_ident, R_cb, R_vecs, R_sc, R_gmix, R_gmlp, R_kmT, R_vm = (Res(n) for n in
            ["ident", "cb", "vecs", "sc", "gmix", "gmlp", "kmT", "vm"])
        R_halo = [Res("halo%d" % i) for i in range(8)]
        R_rsd = [Res("rsd%d" % i) for i in range(64)]
        R_scc = [Res("scc%d" % i) for i in range(64)]
        psall = es.enter_context(nc.psum_tensor("psall", [128, 4096], F32))
        ps = [psall[:, i * 512:(i + 1) * 512] for i in range(8)]
        RP = [Res("ps%d" % i) for i in range(8)]

        tri = cb[:, 0:128]
        blk = cb[:, 128:256]
        ones = cb[:, 256:384]
        identB = cb[:, 384:512]
        negtri = cb[:, 512:640]

        def Bv(i, a=0, b=512):
            return AB[:, i * 512 + a:i * 512 + b]

        def Fv(i, a=0, b=512):
            return AB[:, i * 512:(i + 2) * 512].bitcast(F32)[:, a:b]

        def F4(i, a=0, b=1024):
            return AB[:, i * 512:(i + 4) * 512].bitcast(F32)[:, a:b]

        def rs(i, n=1):
            return RS[i:i + n]

        def dma(q, out_ap, in_ap, reads=(), writes=(), deps=()):
            return P.op(q, lambda e: e.dma_start(out=out_ap, in_=in_ap), reads, writes, deps, dma=True)

        def mm(out_ap, lhsT, rhs, start, stop, reads, writes, sgc=False):
            return P.op("pe", lambda e: e.matmul(out_ap, lhsT=lhsT, rhs=rhs, start=start, stop=stop,
                                                 skip_group_check=sgc), reads, writes)

        def tr(out_ap, in_ap, ident_ap, reads, writes):
            return P.op("pe", lambda e: e.transpose(out=out_ap, in_=in_ap, identity=ident_ap), reads, writes)

        def act(out_ap, in_ap, func, reads, writes, bias=None, scale=None, accum=None):
            kw = {}
            if bias is not None:
                kw["bias"] = bias
            if scale is not None:
                kw["scale"] = scale
            if accum is not None:
                kw["accum_out"] = accum
            return P.op("act", lambda e: e.activation(out=out_ap, in_=in_ap, func=func, **kw), reads, writes)

        def tcopy(eng, out_ap, in_ap, reads, writes):
            return P.op(eng, lambda e: e.tensor_copy(out=out_ap, in_=in_ap), reads, writes)

        def tt(eng, out_ap, a, b, op, reads, writes):
            return P.op(eng, lambda e: e.tensor_tensor(out=out_ap, in0=a, in1=b, op=op), reads, writes)

        def tsc(eng, out_ap, a, s1, op0, reads, writes, s2=None, op1=None):
            if op1 is None:
                return P.op(eng, lambda e: e.tensor_scalar(out=out_ap, in0=a, scalar1=s1, scalar2=None, op0=op0),
                            reads, writes)
            return P.op(eng, lambda e: e.tensor_scalar(out=out_ap, in0=a, scalar1=s1, scalar2=s2, op0=op0, op1=op1),
                        reads, writes)

        def stt(out_ap, a, s, b, op0, op1, reads, writes):
            return P.op("dve", lambda e: e.scalar_tensor_tensor(out=out_ap, in0=a, scalar=s, in1=b, op0=op0, op1=op1),
                        reads, writes)

        def memset(eng, ap, val, writes):
            return P.op(eng, lambda e: e.memset(ap, val), (), writes)

        class Ring:
            def __init__(self, items):
                self.items = list(items)
                self.i = 0

            def next(self):
                v = self.items[self.i % len(self.items)]
                self.i += 1
                return v

        out_dmas = []

        dma("sp", identF[:], consts[:, 0:128], writes=[R_ident])
        dma("pool", cb[:], consts[:, 128:768], writes=[R_cb])
        dma("sp", gmix[:], norm_mix_g.partition_broadcast(128), writes=[R_gmix])
        dma("sp", gmlp[:], norm_mlp_g.partition_broadcast(128), writes=[R_gmlp])
        memset("dve", sc[:, 0:1], EPS, [R_scc[0]])
        dma("sp", gsubb[:], subln_g.partition_broadcast(128), writes=[R_gsubb])
        tsc("dve", gsubb[:], gsubb[:], 0.8, ALU.mult, [R_gsubb], [R_gsubb])
        memset("dve", halo[:], 0.0, R_halo)

        STG = 162
        stg_full = AB[:, STG * 512:(STG + 1) * 512].bitcast(F32)
        memset("dve", stg_full[:, 0:128], 0.0, rs(STG))

        def stg_rows(r0, r1, c0=0, c1=128):
            return AB[r0:r1, STG * 512:(STG + 1) * 512].bitcast(F32)[:, c0:c1]

        dma("sp", stg_rows(0, 24), b_gate.rearrange("(r c) -> r c", c=128), writes=rs(STG))
        dma("sp", stg_rows(24, 48), conv_w.rearrange("j (c p) -> (j c) p", p=128), writes=rs(STG))
        dma("sp", stg_rows(48, 50), mq_norm_g.rearrange("(r c) -> r c", c=128), writes=rs(STG))
        dma("sp", stg_rows(50, 52), mk_norm_g.rearrange("(r c) -> r c", c=128), writes=rs(STG))
        dma("sp", stg_rows(52, 53, 0, 64), q_norm_g.rearrange("(o n) -> o n", o=1), writes=rs(STG))
        dma("sp", stg_rows(52, 53, 64, 128), q_norm_g.rearrange("(o n) -> o n", o=1), writes=rs(STG))
        dma("sp", stg_rows(53, 54, 0, 64), k_norm_g.rearrange("(o n) -> o n", o=1), writes=rs(STG))
        dma("sp", stg_rows(53, 54, 64, 128), k_norm_g.rearrange("(o n) -> o n", o=1), writes=rs(STG))
        dma("sp", stg_rows(54, 55), subln_g.rearrange("(o n) -> o n", o=1), writes=rs(STG))
        tr(ps[0][:, 0:64], stg_rows(0, 64), identF[0:64, 0:64], rs(STG) + [R_ident], [RP[0]])
        tcopy("dve", vecs[:], ps[0][:, 0:64], [RP[0]], [R_vecs])
        tsc("dve", vecs[:, 55:56], vecs[:, 52:53], 0.125, ALU.mult, [R_vecs], [R_vecs])
        tsc("dve", vecs[:, 56:57], vecs[:, 54:55], 0.8, ALU.mult, [R_vecs], [R_vecs])
        tsc("dve", vecs[:, 57:59], vecs[:, 48:50], 1.0 / 16.0, ALU.mult, [R_vecs], [R_vecs])
        V_BG = lambda n: vecs[:, n:n + 1]
        V_CW = lambda j, c: vecs[:, 24 + j * 8 + c:25 + j * 8 + c]
        V_MKG = lambda dc: vecs[:, 50 + dc:51 + dc]
        V_MQG16 = lambda dc: vecs[:, 57 + dc:58 + dc]
        V_GK = vecs[:, 53:54]
        V_GQ8 = vecs[:, 55:56]
        V_SUB08 = vecs[:, 56:57]
        EPSC = sc[:, 0:1]
        LAM = sc[:, 1:2]
        NEGLAM = sc[:, 2:3]
        NEGM = sc[:, 3:4]
        NEGMM = sc[:, 4:5]

        BC = 156
        bc = AB[:, BC * 512:(BC + 2) * 512].bitcast(F32)
        for i, v in enumerate([lam_q1, lam_k1, lam_q2, lam_k2, q_norm_g, k_norm_g]):
            dma("sp", bc[:, i * 64:(i + 1) * 64], v.partition_broadcast(128), writes=rs(BC, 2))
        MB = 158
        mb = AB[:, MB * 512:(MB + 4) * 512].bitcast(F32)
        dma("sp", mb[:, 0:256], mq_norm_g.partition_broadcast(128), writes=rs(MB, 4))
        dma("sp", mb[:, 256:512], mk_norm_g.partition_broadcast(128), writes=rs(MB, 4))
        tt("dve", bc[:, 384:448], bc[:, 0:64], bc[:, 64:128], ALU.mult, rs(BC, 2), rs(BC, 2))
        tt("dve", bc[:, 448:512], bc[:, 128:192], bc[:, 192:256], ALU.mult, rs(BC, 2), rs(BC, 2))
        P.op("dve", lambda e: e.tensor_reduce(out=sc[:, 5:6], in_=bc[:, 384:448], axis=AX.X, op=ALU.add),
             rs(BC, 2), [R_scc[5]])
        P.op("dve", lambda e: e.tensor_reduce(out=sc[:, 6:7], in_=bc[:, 448:512], axis=AX.X, op=ALU.add),
             rs(BC, 2), [R_scc[6]])
        act(sc[:, 7:9], sc[:, 5:7], AF.Exp, [R_scc[5], R_scc[6]], [R_scc[7], R_scc[8]])
        tt("dve", sc[:, 9:10], sc[:, 7:8], sc[:, 8:9], ALU.subtract, [R_scc[7], R_scc[8]], [R_scc[9]])
        tsc("dve", sc[:, 1:2], sc[:, 9:10], 0.2, ALU.add, [R_scc[9]], [R_scc[1]])
        tsc("dve", sc[:, 2:3], sc[:, 1:2], -1.0, ALU.mult, [R_scc[1]], [R_scc[2]])
        P.op("dve", lambda e: e.tensor_reduce(out=sc[:, 10:11], in_=bc[:, 256:320], axis=AX.X, op=ALU.max,
                                              apply_absolute_value=True), rs(BC, 2), [R_scc[10]])
        P.op("dve", lambda e: e.tensor_reduce(out=sc[:, 11:12], in_=bc[:, 320:384], axis=AX.X, op=ALU.max,
                                              apply_absolute_value=True), rs(BC, 2), [R_scc[11]])
        tt("dve", sc[:, 12:13], sc[:, 10:11], sc[:, 11:12], ALU.mult, [R_scc[10], R_scc[11]], [R_scc[12]])
        tsc("dve", sc[:, 3:4], sc[:, 12:13], -8.0, ALU.mult, [R_scc[12]], [R_scc[3]])
        P.op("dve", lambda e: e.tensor_reduce(out=sc[:, 13:14], in_=mb[:, 0:256], axis=AX.X, op=ALU.max,
                                              apply_absolute_value=True), rs(MB, 4), [R_scc[13]])
        P.op("dve", lambda e: e.tensor_reduce(out=sc[:, 14:15], in_=mb[:, 256:512], axis=AX.X, op=ALU.max,
                                              apply_absolute_value=True), rs(MB, 4), [R_scc[14]])
        tt("dve", sc[:, 15:16], sc[:, 13:14], sc[:, 14:15], ALU.mult, [R_scc[13], R_scc[14]], [R_scc[15]])
        tsc("dve", sc[:, 4:5], sc[:, 15:16], -16.0, ALU.mult, [R_scc[15]], [R_scc[4]])

        rsd_ctr = [0]
        psA = Ring([0, 1, 2, 3])

        def norm_stage1(xt_ap, xt_res, h_ap, h_res, g_ap, g_res, split=False, junk=None):
            i = rsd_ctr[0] % 32
            rsd_ctr[0] += 1
            c0, c1 = 2 * i, 2 * i + 1
            if junk is None:
                act(h_ap, xt_ap, AF.Square, xt_res, h_res + [R_rsd[c0]], accum=rsd[:, c0:c0 + 1])
            else:
                act(junk[0], xt_ap, AF.Square, xt_res, junk[1] + [R_rsd[c0]], accum=rsd[:, c0:c0 + 1])
            act(rsd[:, c1:c1 + 1], rsd[:, c0:c0 + 1], AF.Ln, [R_rsd[c0], R_scc[0]], [R_rsd[c1]],
                bias=EPSC, scale=1.0 / D)
            act(rsd[:, c0:c0 + 1], rsd[:, c1:c1 + 1], AF.Exp, [R_rsd[c1]], [R_rsd[c0]], scale=-0.5)
            if split:
                act(h_ap, xt_ap, AF.Copy, xt_res + [R_rsd[c0]], h_res, scale=rsd[:, c0:c0 + 1])
                tt("pool", h_ap, h_ap, g_ap, ALU.mult, h_res + [g_res], h_res)
            else:
                stt(h_ap, xt_ap, rsd[:, c0:c0 + 1], g_ap, ALU.mult, ALU.mult,
                    xt_res + [R_rsd[c0], g_res], h_res)

        def norm_stage2(h_ap, h_res, dst3_fn, dst_res_fn):
            for half in range(2):
                b = psA.next()
                for j in range(4):
                    kc = half * 4 + j
                    tr(ps[b][:, j * 128:(j + 1) * 128], h_ap[:, kc * 128:(kc + 1) * 128], identF[:],
                       h_res + [R_ident], [RP[b]])
                if "A" in SKIP:
                    d3 = dst3_fn(half)
                    for j in range(4):
                        tcopy("dve", d3[:, j, :], ps[b][:, j * 128:(j + 1) * 128], [RP[b]], dst_res_fn(half))
                else:
                    tcopy("dve", dst3_fn(half), ps[b][:, :].rearrange("p (k n) -> p k n", n=128), [RP[b]],
                          dst_res_fn(half))

        def norm_transpose(xt_ap, xt_res, h_ap, h_res, g_ap, g_res, dst3_fn, dst_res_fn, split=False, junk=None):
            norm_stage1(xt_ap, xt_res, h_ap, h_res, g_ap, g_res, split, junk)
            norm_stage2(h_ap, h_res, dst3_fn, dst_res_fn)

        MX = [100, 104]
        GM = 108
        MH = [112, 116]
        MN = 120
        WKV = 124
        dma("sp", F4(GM), norm_mem_g.partition_broadcast(128), writes=rs(GM, 4))
        dma("pool", AB[:, WKV * 512:(WKV + 32) * 512].rearrange("p (k n) -> p k n", n=2048),
            w_mem_kv.rearrange("(k p) n -> p k n", p=128), writes=rs(WKV, 32))

        def memnT(kc, a, b):
            return AB[:, MN * 512 + kc * 256 + a:MN * 512 + kc * 256 + b]

        def wkv(kc, a, b):
            return AB[:, WKV * 512 + kc * 2048 + a:WKV * 512 + kc * 2048 + b]

        def do_mem_kv():
            for mt in range(2 if "memkv" not in SKIP else 0):
                dma("sp", F4(MX[mt]), mem[mt * 128:(mt + 1) * 128, :], writes=rs(MX[mt], 4))
                norm_transpose(F4(MX[mt]), rs(MX[mt], 4), F4(MH[mt]), rs(MH[mt], 4), F4(GM), RS[GM],
                               lambda half, mt=mt: AB[:, MN * 512 + half * 1024:MN * 512 + (half + 1) * 1024]
                               .rearrange("p (k n) -> p k n", n=256)[:, :, mt * 128:(mt + 1) * 128],
                               lambda half: rs(MN, 4))
            KT0 = 164
            for hd in range(4 if ("memkv" not in SKIP and "memk" not in SKIP) else 0):
                for dc in range(2):
                    c = hd * 2 + dc
                    b = psA.next()
                    for kc in range(8):
                        mm(ps[b][:, 0:256], wkv(kc, c * 128, (c + 1) * 128), memnT(kc, 0, 256), kc == 0, kc == 7,
                           rs(WKV, 32) + rs(MN, 4), [RP[b]])
                    tcopy("dve", Fv(164 + 2 * dc, 0, 256), ps[b][:, 0:256], [RP[b]], rs(164 + 2 * dc, 2))
                    act(Bv(168 + dc, 0, 256), Fv(164 + 2 * dc, 0, 256), AF.Square, rs(164 + 2 * dc, 2), rs(168 + dc))
                if "K2" in SKIP:
                    continue
                b = psA.next()
                for dc in range(2):
                    mm(ps[b][:, 0:256], ones, Bv(168 + dc, 0, 256), dc == 0, dc == 1, [R_cb] + rs(168 + dc), [RP[b]])
                if "K3" in SKIP:
                    continue
                act(Fv(170, 0, 256), ps[b][:, 0:256], AF.Ln, [RP[b], R_scc[0]], rs(170, 2), bias=EPSC, scale=1.0 / 256)
                act(Fv(172, 0, 256), Fv(170, 0, 256), AF.Exp, rs(170, 2), rs(172, 2), scale=-0.5)
                if "K4" in SKIP:
                    continue
                for dc in range(2):
                    c = hd * 2 + dc
                    stt(kmT[:, c * 256:(c + 1) * 256], Fv(164 + 2 * dc, 0, 256), V_MKG(dc), Fv(172, 0, 256),
                        ALU.mult, ALU.mult, rs(164 + 2 * dc, 2) + rs(172, 2) + [R_vecs], [R_kmT])
            for mt in range(2 if ("memkv" not in SKIP and "memv" not in SKIP) else 0):
                for nh in range(2):
                    b = psA.next()
                    for kc in range(8):
                        mm(ps[b][:, :], memnT(kc, mt * 128, (mt + 1) * 128), wkv(kc, 1024 + nh * 512, 1024 + (nh + 1) * 512),
                           kc == 0, kc == 7, rs(WKV, 32) + rs(MN, 4), [RP[b]])
                    tcopy("dve", vm[:, mt * 1024 + nh * 512:mt * 1024 + (nh + 1) * 512], ps[b][:, :], [RP[b]], [R_vm])


        R_blk = {}
        cast_jobs = []
        for j in range(7):
            R_blk[("in", j)] = Res()
            cast_jobs.append((sc_in[:, j * 1024:(j + 1) * 1024], w_in[:, 3072 + j * 1024:3072 + (j + 1) * 1024],
                              R_blk[("in", j)]))
        for i, wsrc in enumerate([w_attn_o, w_conv_o, w_mem_o, w_o]):
            R_blk[("sq", i)] = Res()
            cast_jobs.append((sc_sq[i], wsrc, R_blk[("sq", i)]))
        for j in range(4):
            R_blk[("w1", j)] = Res()
            cast_jobs.append((sc_w1[:, j * 1024:(j + 1) * 1024], w_mlp_in[:, j * 1024:(j + 1) * 1024], R_blk[("w1", j)]))
        for j in range(4):
            R_blk[("w2", j)] = Res()
            cast_jobs.append((sc_w2[j * 1024:(j + 1) * 1024, :], w_mlp_out[j * 1024:(j + 1) * 1024, :], R_blk[("w2", j)]))
        cast_jobs_small = []
        R_pieces = {}
        for (o_ap, i_ap, r) in cast_jobs:
            R_pieces[id(r)] = [Res() for _ in range(8)]
            for rr in range(8):
                cast_jobs_small.append((o_ap[rr * 128:(rr + 1) * 128, :], i_ap[rr * 128:(rr + 1) * 128, :],
                                        R_pieces[id(r)][rr]))
        cast_jobs = cast_jobs_small
        cast_ptr = [0]
        cast_ops = []

        def issue_casts(n):
            if "casts" in SKIP:
                return
            for _ in range(n):
                if cast_ptr[0] < len(cast_jobs):
                    o_ap, i_ap, r = cast_jobs[cast_ptr[0]]
                    cast_ptr[0] += 1
                    lag = int(os.environ.get("MK_CASTLAG", "4"))
                    dep = cast_ops[-lag] if len(cast_ops) >= lag else None
                    cast_ops.append(dma("pool", o_ap, i_ap, writes=[r], deps=[dep]))

        def hT(kc, a, b):
            return AB[:, kc * 4096 + a:kc * 4096 + b]

        XS = [64, 68, 72, 96]
        HS = [76, 80, 84, 88]
        JUNK = 92
        for t in range(32 if "phaseA" not in SKIP else 0):
            xs_ = XS[t % 4]
            hs_ = HS[t % 4]
            dma("sp", F4(xs_), x[t * 128:(t + 1) * 128, :], writes=rs(xs_, 4))
            norm_transpose(F4(xs_), rs(xs_, 4), F4(hs_), rs(hs_, 4), gmix[:], R_gmix,
                           lambda half, t=t: AB[:, half * 4 * 4096:(half * 4 + 4) * 4096]
                           .rearrange("p (k n) -> p k n", n=4096)[:, :, t * 128:(t + 1) * 128],
                           lambda half, t=t: [RS[kc * 8 + t // 4] for kc in range(half * 4, half * 4 + 4)],
                           split=(t % 2 == 1), junk=(F4(JUNK), rs(JUNK, 4)))
        do_mem_kv()

        QS = [64, 64]
        KS1 = 72
        VS = [96, 96]
        WH = [112, 118]
        PTS = list(range(124, 132))
        SQ = [132, 133, 137]
        OSQ = 134
        OOUT = [135, 136]
        QRAW = [138, 140, 166]
        LNT = [142, 144, 168]
        RSTD = [146, 148, 170]
        VAS = [96, 172]
        ACCS = [150, 152, 154]
        O0S, T1S, ONB = 156, 158, 160
        for VA_ in VAS:
            memset("pool", AB[:, VA_ * 512:VA_ * 512 + 32 * 132].rearrange("p (t c) -> p t c", c=132)[:, :, 128:132],
                   1.0, rs(VA_, 9))

        def wh(par, j, kc):
            base = WH[par] * 512 + j * 1024 + kc * 128
            return AB[:, base:base + 128]

        def load_wh(h):
            par = h % 2
            for j in range(3):
                base = WH[par] * 512 + j * 1024
                dma("pool", AB[:, base:base + 1024].rearrange("p (k n) -> p k n", n=128),
                    w_in[:, j * 1024 + h * 128:j * 1024 + (h + 1) * 128].rearrange("(k p) n -> p k n", p=128),
                    writes=rs(WH[par] + 2 * j, 2))

        if nheads > 0:
            load_wh(0)
        psS = Ring([0, 1, 2, 3])
        pt_ring = Ring(PTS[0::2])
        spair = Ring([0, 2])
        deferred = []

        def run_deferred(n=1000):
            while deferred and n > 0:
                deferred.pop(0)()
                n -= 1
        nrm_i = [0]
        oout_i = [0]
        for h in range(nheads):
            par = h % 2
            pend = []

            def qk_stat_a(info):
                (b, which, c, i) = info
                P.label = "stat w%d c%d" % (which, c)
                b2 = psS.next()
                mm(ps[b2][:, :], blk, Bv(SQ[i]), True, True, [R_cb] + rs(SQ[i]), [RP[b2]])
                act(Fv(LNT[i]), ps[b2][:, :], AF.Ln, [RP[b2], R_scc[0]], rs(LNT[i], 2), bias=EPSC, scale=1.0 / 64)
                act(Fv(RSTD[i]), Fv(LNT[i]), AF.Exp, rs(LNT[i], 2), rs(RSTD[i], 2), scale=-0.5)

            def qk_stat_b(info):
                (b, which, c, i) = info
                if which == 0:
                    dst = QS[par] + c
                    stt(Bv(dst), Fv(QRAW[i]), V_GQ8, Fv(RSTD[i]), ALU.mult, ALU.mult,
                        rs(QRAW[i], 2) + rs(RSTD[i], 2) + [R_vecs], rs(dst))
                else:
                    dst = KS1 + c
                    stt(Bv(dst), Fv(QRAW[i]), V_GK, Fv(RSTD[i]), ALU.mult, ALU.mult,
                        rs(QRAW[i], 2) + rs(RSTD[i], 2) + [R_vecs], rs(dst))

            for which in range(2):
                for c in range(8):
                    i = nrm_i[0] % 3
                    nrm_i[0] += 1
                    old_info = pend.pop(0) if len(pend) >= 2 else None
                    if old_info is not None:
                        qk_stat_a(old_info)
                    b = psS.next()
                    P.label = "proj h%d w%d c%d" % (h, which, c)
                    for kc in range(8):
                        mm(ps[b][:, :], wh(par, which, kc), hT(kc, c * 512, (c + 1) * 512), kc == 0, kc == 7,
                           rs(WH[par], 6) + rs(kc * 8 + c), [RP[b]])
                    tcopy("dve", Fv(QRAW[i]), ps[b][:, :], [RP[b]], rs(QRAW[i], 2))
                    tt("pool", Bv(SQ[i]), Fv(QRAW[i]), Fv(QRAW[i]), ALU.mult, rs(QRAW[i], 2), rs(SQ[i]))
                    if old_info is not None:
                        qk_stat_b(old_info)
                    pend.append((b, which, c, i))
                    if c in (0, 2, 4, 6):
                        run_deferred(1)
            while pend:
                info = pend.pop(0)
                qk_stat_a(info)
                qk_stat_b(info)
            VA = VAS[par]

            def vproj_unit(hh, g, j, b):
                pp = hh % 2
                t = g * 4 + j
                for kc in range(8):
                    mm(ps[b][:, j * 128:(j + 1) * 128], hT(kc, t * 128, (t + 1) * 128), wh(pp, 2, kc),
                       kc == 0, kc == 7, rs(WH[pp], 6) + rs(kc * 8 + g), [RP[b]])
                if j == 3:
                    va = VAS[pp]
                    tcopy("dve", AB[:, va * 512 + g * 528:va * 512 + (g + 1) * 528].rearrange(
                        "p (t c) -> p t c", c=132)[:, :, 0:128],
                        ps[b][:, :].rearrange("p (t c) -> p t c", c=128), [RP[b]], rs(va, 9))

            def vproj_half(hh, g, j, hf, b):
                pp = hh % 2
                t = g * 4 + j
                for kc in range(hf * 4, hf * 4 + 4):
                    mm(ps[b][:, j * 128:(j + 1) * 128], hT(kc, t * 128, (t + 1) * 128), wh(pp, 2, kc),
                       kc == 0, kc == 7, rs(WH[pp], 6) + rs(kc * 8 + g), [RP[b]])
                if j == 3 and hf == 1:
                    va = VAS[pp]
                    tcopy("dve", AB[:, va * 512 + g * 528:va * 512 + (g + 1) * 528].rearrange(
                        "p (t c) -> p t c", c=132)[:, :, 0:128],
                        ps[b][:, :].rearrange("p (t c) -> p t c", c=128), [RP[b]], rs(va, 9))

            if h == 0:
                for g in range(8):
                    b = psS.next()
                    for j in range(4):
                        vproj_unit(0, g, j, b)
            if h + 1 < nheads:
                load_wh(h + 1)
                vunits = [(g, j, hf) for g in range(8) for j in range(4) for hf in range(2)]
            else:
                vunits = []
            ktctr = [0]
            def qk(qc, kt):
                qslot = QS[par] + qc
                j = kt - 4 * qc
                off = max(0, j) * 128
                ko = (kt % 4) * 128
                sb_ = []
                pair = spair.next()
                kslot = KS1 + kt // 4
                for sub in range(2):
                    b = pair + sub
                    p0, p1 = sub * 64, (sub + 1) * 64
                    mm(ps[b][:, off:512], AB[p0:p1, kslot * 512 + ko:kslot * 512 + ko + 128],
                       AB[p0:p1, qslot * 512 + off:qslot * 512 + 512], True, j < 0,
                       rs(kslot) + rs(qslot), [RP[b]])
                    sb_.append(b)
                if j >= 0:
                    for sub in range(2):
                        b = pair + sub
                        mm(ps[b][:, off:off + 128], identB, negtri, False, True, [R_cb], [RP[b]])
                return sb_

            def softmax(qc, kt, sb_):
                j = kt - 4 * qc
                off = max(0, j) * 128
                p0 = pt_ring.next()
                pts = [p0, p0 + 1]
                b0 = sb_[0]
                in3 = psall[:, b0 * 512:(b0 + 2) * 512].rearrange("p (s n) -> p s n", n=512)[:, :, off:512]
                out3 = AB[:, p0 * 512:(p0 + 2) * 512].rearrange("p (s n) -> p s n", n=512)[:, :, off:512]
                act(out3, in3, AF.Exp, [RP[b0], RP[b0 + 1], R_scc[3]], rs(p0, 2), bias=NEGM, scale=1.0)
                return pts

            def pv(qc, kt, pts, first_step, last_step):
                j = kt - 4 * qc
                q0 = max(0, j)
                vcol = VA * 512 + kt * 132
                for sub in range(2):
                    for qt in range(q0, 4):
                        a_ = sub * 4 + qt
                        bank = 4 + a_ // 3
                        col = (a_ % 3) * 129
                        mm(ps[bank][:, col:col + 129], Bv(pts[sub], qt * 128, (qt + 1) * 128),
                           AB[:, vcol:vcol + 129], first_step and a_ % 3 == 0, last_step,
                           rs(VA, 9) + rs(pts[sub]), [RP[bank]], sgc=True)

            steps = []
            for qc in range(nqc):
                fulls = list(range(4 * qc))
                diags = list(range(4 * qc, 4 * qc + 4))
                order = []
                if fulls:
                    per = max(1, len(fulls) // 4)
                    di = 0
                    for fi, kt_ in enumerate(fulls):
                        order.append(kt_)
                        if (fi + 1) % per == 0 and di < 4:
                            order.append(diags[di])
                            di += 1
                    order.extend(diags[di:])
                else:
                    order = diags
                for oi, kt_ in enumerate(order):
                    steps.append((qc, kt_, oi, len(order)))
            cur_pts = softmax(*steps[0][:2], qk(*steps[0][:2])) if steps else None
            for si, (qc, kt, oi, on_) in enumerate(steps):
                nkt = 4 * qc + 4
                if oi == 0:
                    issue_casts(3)
                nxt = steps[si + 1][:2] if si + 1 < len(steps) else None
                sb_n = qk(*nxt) if nxt is not None else None
                pv(qc, kt, cur_pts, oi == 0, oi == on_ - 1)
                last = oi == on_ - 1
                if last:
                    epilogue_now = True
                else:
                    epilogue_now = False
                if epilogue_now:
                    pass
                if not last:
                    if nxt is not None:
                        cur_pts = softmax(*nxt, sb_n)
                    if oi in (0, 2, 4, 6, 8):
                        run_deferred(1)
                    ktctr[0] += 1
                    if vunits and ktctr[0] % 2 == 1:
                        g_, j_, half_ = vunits.pop(0)
                        vproj_half(h + 1, g_, j_, half_, 7)
                    continue
                run_deferred()
                for bk in range(3):
                    n_ = 387 if bk < 2 else 258
                    tcopy("dve", Fv(ACCS[bk], 0, n_), ps[4 + bk][:, 0:n_], [RP[4 + bk]], rs(ACCS[bk], 2))

                def accv(a_, c0, c1):
                    return Fv(ACCS[a_ // 3], (a_ % 3) * 129 + c0, (a_ % 3) * 129 + c1)

                def epA():
                    for bk in range(3):
                        n3 = 3 if bk < 2 else 2
                        src = Fv(ACCS[bk], 0, n3 * 129).rearrange("p (a c) -> p a c", c=129)[:, :, 128:129]
                        P.op("dve", lambda e, o_=epi[:, 3 * bk:3 * bk + n3], i_=src: e.reciprocal(out=o_, in_=i_),
                             rs(ACCS[bk], 2), [R_epi[0]])
                    for qt in range(4):
                        tsc("dve", Fv(O0S, qt * 128, (qt + 1) * 128), accv(qt, 0, 128), epi[:, qt:qt + 1], ALU.mult,
                            rs(ACCS[qt // 3], 2) + [R_epi[0]], rs(O0S, 2))
                        tsc("dve", Fv(T1S, qt * 128, (qt + 1) * 128), accv(4 + qt, 0, 128), epi[:, 4 + qt:5 + qt],
                            ALU.mult, rs(ACCS[(4 + qt) // 3], 2) + [R_epi[0]], rs(T1S, 2))

                def epB():
                    stt(Fv(O0S), Fv(T1S), NEGLAM, Fv(O0S), ALU.mult, ALU.add,
                        rs(T1S, 2) + rs(O0S, 2) + [R_scc[2]], rs(O0S, 2))
                    tt("dve", Fv(T1S), Fv(O0S), Fv(O0S), ALU.mult, rs(O0S, 2), rs(T1S, 2))
                    P.op("dve", lambda e: e.tensor_reduce(out=epi[:, 8:12],
                                                          in_=Fv(T1S).rearrange("p (a c) -> p a c", c=128),
                                                          axis=AX.X, op=ALU.add), rs(T1S, 2), [R_epi[1]])

                def epC():
                    act(epi[:, 12:16], epi[:, 8:12], AF.Ln, [R_epi[1], R_scc[0]], [R_epi[2]], bias=EPSC, scale=1.0 / 128)
                    act(epi[:, 16:20], epi[:, 12:16], AF.Exp, [R_epi[2]], [R_epi[3]], scale=-0.5)
                    for qt in range(4):
                        stt(Fv(ONB, qt * 128, (qt + 1) * 128), Fv(O0S, qt * 128, (qt + 1) * 128),
                            epi[:, 16 + qt:17 + qt], gsubb[:], ALU.mult, ALU.mult,
                            rs(O0S, 2) + [R_epi[3], R_gsubb], rs(ONB, 2))

                R_o = Res()
                R_blk[("oT", h, qc)] = R_o

                def epD(h=h, qc=qc, R_o=R_o):
                    b2 = psS.next()
                    for qt in range(4):
                        tr(ps[b2][:, qt * 128:(qt + 1) * 128], Fv(ONB, qt * 128, (qt + 1) * 128), identF[:],
                           rs(ONB, 2) + [R_ident], [RP[b2]])
                    oo = OOUT[oout_i[0] % 2]
                    oout_i[0] += 1
                    tcopy("dve", Bv(oo), ps[b2][:, :], [RP[b2]], rs(oo))
                    dma("pool", oTd[h * 128:(h + 1) * 128, qc * 512:(qc + 1) * 512], Bv(oo), reads=rs(oo), writes=[R_o])

                deferred.extend([epA, epB, epC, epD])
                if nxt is not None:
                    cur_pts = softmax(*nxt, sb_n)
            while vunits:
                g_, j_, half_ = vunits.pop(0)
                vproj_half(h + 1, g_, j_, half_, 7)
        run_deferred()
        issue_casts(len(cast_jobs))

        XIN = [[0 + 16 * s + 4 * t for t in range(4)] for s in range(2)]
        HF = [32, 36]
        HT = 40
        OTC = [48, 56]
        CONVT = 64
        OMT = 72
        QMT = 80
        MRGT = 88
        AT = 64
        MRG = 96
        WR = [112, 120, 128, 136]
        TMP = 144
        XCS = [TMP + 2 * i for i in range(4)]
        INN = [TMP + 8 + 3 * i for i in range(4)]
        YT = [TMP + 20, TMP + 22]
        GT = [TMP + 24 + 2 * i for i in range(4)]
        MT = [TMP + 32 + 2 * i for i in range(5)]

        def inn(i, a, b):
            return AB[:, INN[i] * 512:(INN[i] + 3) * 512].bitcast(F32)[:, a:b]

        groups = []
        for T in range(nchunks):
            gl = []
            for j in range(2):
                for kind, cb0 in (("xc", 0), ("gc", 2048), ("gb", 1024)):
                    gl.append((kind, j, sc_in, 0, cb0 + j * 512, R_blk[("in", (cb0 + j * 512) // 1024)]))
            def gate_bo(br, j):
                gl.append(("gate", (br, j), sc_in, 0, 4096 + br * 1024 + j * 512, R_blk[("in", 4 + br)]))
                gl.append(("bo", (br, j), sc_sq[br], 0, j * 512, R_blk[("sq", br)]))
            gl.append(("qm", 0, sc_in, 0, 3072, R_blk[("in", 3)]))
            gate_bo(0, 0)
            gl.append(("qm", 1, sc_in, 0, 3072 + 512, R_blk[("in", 3)]))
            gate_bo(0, 1)
            for br in (1, 2):
                for j in range(2):
                    gate_bo(br, j)
            for j in range(2):
                gl.append(("wo", j, sc_sq[3], 0, j * 512, R_blk[("sq", 3)]))
            for j in range(8):
                gl.append(("w1", j, sc_w1, 0, j * 512, R_blk[("w1", j // 2)]))
            for nh in range(2):
                for fb in range(4):
                    gl.append(("w2", (nh, fb), sc_w2, fb * 1024, nh * 512, R_blk[("w2", fb)]))
            groups.extend(gl)
        gptr = [0]
        gissued = [0]

        def issue_groups(upto):
            while gissued[0] < min(upto, len(groups)):
                g = gissued[0]
                kind, idx, scr, r0, c0, rb = groups[g]
                slot = WR[g % 4]
                dma("sp", AB[:, slot * 512:(slot + 8) * 512].rearrange("p (k n) -> p k n", n=512),
                    scr[r0:r0 + 1024, c0:c0 + 512].rearrange("(k p) n -> p k n", p=128),
                    reads=R_pieces[id(rb)], writes=rs(slot, 8))
                gissued[0] += 1

        def next_group(kind_expect, hold=0):
            g = gptr[0]
            gptr[0] += 1
            issue_groups(g + 4 - hold)
            kind, idx, scr, r0, c0, rb = groups[g]
            assert kind == kind_expect, (kind, kind_expect)
            slot = WR[g % 4]

            def wv(kc, a, b):
                return AB[:, slot * 512 + kc * 512 + a:slot * 512 + kc * 512 + b]
            return idx, wv, rs(slot, 8)

        def load_chunk_inputs(T):
            s = T % 2
            for t in range(4):
                dma("sp", F4(XIN[s][t]), x[T * 512 + t * 128:T * 512 + (t + 1) * 128, :], writes=rs(XIN[s][t], 4))
            dma("sp", AB[:, OTC[s] * 512:(OTC[s] + 8) * 512].rearrange("p (k n) -> p k n", n=512),
                oTd[:, T * 512:(T + 1) * 512].rearrange("(k p) n -> p k n", p=128),
                reads=[R_blk[("oT", hh, T)] for hh in range(nheads) if ("oT", hh, T) in R_blk],
                writes=rs(OTC[s], 8))

        psC = Ring([0, 1, 2, 3, 4, 5, 6, 7])
        hf_i = [0]
        if nchunks > 0:
            load_chunk_inputs(0)
        for T in range(nchunks):
            s = T % 2
            if T + 1 < nchunks:
                load_chunk_inputs(T + 1)
            issue_groups(gptr[0] + 3)
            HFB = [32, 36, 144, 148]

            def ht_dst3(half, t):
                return AB[:, (HT + half * 4) * 512:(HT + half * 4 + 4) * 512].rearrange(
                    "p (k n) -> p k n", n=512)[:, :, t * 128:(t + 1) * 128]

            if T == 0 or "D" in SKIP:
                for t in range(4):
                    norm_transpose(F4(XIN[s][t]), rs(XIN[s][t], 4), F4(HFB[t]), rs(HFB[t], 4), gmix[:], R_gmix,
                                   lambda half, t=t: ht_dst3(half, t), lambda half: rs(HT + half * 4, 4))

            def proj(wv, wres, j, srcslot):
                b = psC.next()
                for kc in range(8):
                    mm(ps[b][:, :], wv(kc, j * 128, (j + 1) * 128), Bv(srcslot + kc), kc == 0, kc == 7,
                       wres + rs(srcslot + kc), [RP[b]])
                return b

            for half in range(2):
                idx, wv, wres = next_group("xc")
                for j in range(4):
                    b = proj(wv, wres, j, HT)
                    tcopy("dve", Fv(XCS[j]), ps[b][:, :], [RP[b]], rs(XCS[j], 2))
                idx, wv, wres = next_group("gc")
                for j in range(4):
                    c = half * 4 + j
                    b = proj(wv, wres, j, HT)
                    tcopy("dve", inn(j, 0, 2), halo[:, 2 * c:2 * c + 2], [R_halo[c]], rs(INN[j], 3))
                    tt("dve", inn(j, 2, 514), ps[b][:, :], Fv(XCS[j]), ALU.mult, [RP[b]] + rs(XCS[j], 2), rs(INN[j], 3))
                    tcopy("dve", halo[:, 2 * c:2 * c + 2], inn(j, 512, 514), rs(INN[j], 3), [R_halo[c]])
                idx, wv, wres = next_group("gb")
                for j in range(4):
                    c = half * 4 + j
                    y = YT[j % 2]
                    tsc("dve", Fv(y), inn(j, 0, 512), V_CW(0, c), ALU.mult, rs(INN[j], 3) + [R_vecs], rs(y, 2))
                    stt(Fv(y), inn(j, 1, 513), V_CW(1, c), Fv(y), ALU.mult, ALU.add, rs(INN[j], 3) + rs(y, 2) + [R_vecs], rs(y, 2))
                    stt(Fv(y), inn(j, 2, 514), V_CW(2, c), Fv(y), ALU.mult, ALU.add, rs(INN[j], 3) + rs(y, 2) + [R_vecs], rs(y, 2))
                    b = proj(wv, wres, j, HT)
                    tt("dve", Bv(CONVT + c), ps[b][:, :], Fv(y), ALU.mult, [RP[b]] + rs(y, 2), rs(CONVT + c))
            QRAW_M = [[144, 146], [148, 150]]
            SQ_M = [[164, 165], [166, 167]]
            LN_M = [152, 154]
            RS_M = [156, 158]
            PTM = [[160, 161], [162, 163]]
            RCP = [176, 178]

            def mS1(hd, wv, wres, hl):
                p = hd % 2
                for dc in range(2):
                    b = proj(wv, wres, hl * 2 + dc, HT)
                    tcopy("dve", Fv(QRAW_M[p][dc]), ps[b][:, :], [RP[b]], rs(QRAW_M[p][dc], 2))
                    tt("pool", Bv(SQ_M[p][dc]), Fv(QRAW_M[p][dc]), Fv(QRAW_M[p][dc]), ALU.mult,
                       rs(QRAW_M[p][dc], 2), rs(SQ_M[p][dc]))

            def mS2(hd):
                p = hd % 2
                b = psC.next()
                for dc in range(2):
                    mm(ps[b][:, :], ones, Bv(SQ_M[p][dc]), dc == 0, dc == 1, [R_cb] + rs(SQ_M[p][dc]), [RP[b]])
                act(Fv(LN_M[p]), ps[b][:, :], AF.Ln, [RP[b], R_scc[0]], rs(LN_M[p], 2), bias=EPSC, scale=1.0 / 256)
                act(Fv(RS_M[p]), Fv(LN_M[p]), AF.Exp, rs(LN_M[p], 2), rs(RS_M[p], 2), scale=-0.5)
                for dc in range(2):
                    stt(Bv(QMT + hd * 2 + dc), Fv(QRAW_M[p][dc]), V_MQG16(dc), Fv(RS_M[p]), ALU.mult, ALU.mult,
                        rs(QRAW_M[p][dc], 2) + rs(RS_M[p], 2) + [R_vecs], rs(QMT + hd * 2 + dc))

            def mS3(hd):
                p = hd % 2
                for mt in range(2):
                    b = psC.next()
                    for dc in range(2):
                        c = hd * 2 + dc
                        mm(ps[b][:, :], kmT[:, c * 256 + mt * 128:c * 256 + (mt + 1) * 128], Bv(QMT + c),
                           dc == 0, dc == 1, [R_kmT] + rs(QMT + c), [RP[b]])
                    act(Bv(PTM[p][mt]), ps[b][:, :], AF.Exp, [RP[b], R_scc[4]], rs(PTM[p][mt]), bias=NEGMM, scale=1.0)
                bl = psC.next()
                for mt in range(2):
                    mm(ps[bl][:, :], ones, Bv(PTM[p][mt]), mt == 0, mt == 1, [R_cb] + rs(PTM[p][mt]), [RP[bl]])
                tcopy("dve", Fv(RCP[p]), ps[bl][:, :], [RP[bl]], rs(RCP[p], 2))
                P.op("dve", lambda e, a=Fv(RCP[p]): e.reciprocal(out=a, in_=a), rs(RCP[p], 2), rs(RCP[p], 2))

            def mS4(hd):
                p = hd % 2
                for ec in range(2):
                    b = psC.next()
                    for mt in range(2):
                        col = mt * 1024 + hd * 256 + ec * 128
                        mm(ps[b][:, :], vm[:, col:col + 128], Bv(PTM[p][mt]), mt == 0, mt == 1,
                           [R_vm] + rs(PTM[p][mt]), [RP[b]])
                    tt("dve", Bv(OMT + hd * 2 + ec), ps[b][:, :], Fv(RCP[p]), ALU.mult,
                       [RP[b]] + rs(RCP[p], 2), rs(OMT + hd * 2 + ec))

            srcs = [OTC[s], CONVT, OMT]

            def gate_block(br, half):
                idx, wv, wres = next_group("gate")
                for j in range(4):
                    n = br * 8 + half * 4 + j
                    b = proj(wv, wres, j, HT)
                    act(Fv(GT[j]), ps[b][:, :], AF.Sigmoid, [RP[b], R_vecs], rs(GT[j], 2), bias=V_BG(n), scale=1.0)
                idx, wv, wres = next_group("bo")
                for j in range(4):
                    f = half * 4 + j
                    b = proj(wv, wres, j, srcs[br])
                    if br == 0:
                        tt("dve", Fv(MRG + 2 * f), ps[b][:, :], Fv(GT[j]), ALU.mult,
                           [RP[b]] + rs(GT[j], 2), rs(MRG + 2 * f, 2))
                    else:
                        tt("dve", Fv(GT[j]), ps[b][:, :], Fv(GT[j]), ALU.mult, [RP[b]] + rs(GT[j], 2), rs(GT[j], 2))
                        if br == 1:
                            tt("pool", Fv(MRG + 2 * f), Fv(MRG + 2 * f), Fv(GT[j]), ALU.add,
                               rs(MRG + 2 * f, 2) + rs(GT[j], 2), rs(MRG + 2 * f, 2))
                        else:
                            tt("pool", Bv(MRGT + f), Fv(MRG + 2 * f), Fv(GT[j]), ALU.add,
                               rs(MRG + 2 * f, 2) + rs(GT[j], 2), rs(MRGT + f))

            if "C" in SKIP:
                idx, wv, wres = next_group("qm")
                for hd in (0, 1):
                    mS1(hd, wv, wres, hd)
                    mS2(hd)
                    mS3(hd)
                    mS4(hd)
                gate_block(0, 0)
                idx, wv, wres = next_group("qm")
                for hd in (2, 3):
                    mS1(hd, wv, wres, hd - 2)
                    mS2(hd)
                    mS3(hd)
                    mS4(hd)
                gate_block(0, 1)
                gate_block(1, 0)
                gate_block(1, 1)
            else:
                idx, wv, wres = next_group("qm")
                mS1(0, wv, wres, 0)
                mS1(1, wv, wres, 1)
                gate_block(0, 0)
                mS2(0)
                mS2(1)
                idx, wv, wres = next_group("qm")
                mS1(2, wv, wres, 0)
                mS1(3, wv, wres, 1)
                gate_block(0, 1)
                mS3(0)
                mS3(1)
                mS2(2)
                mS2(3)
                gate_block(1, 0)
                mS4(0)
                mS4(1)
                mS3(2)
                mS3(3)
                gate_block(1, 1)
                mS4(2)
                mS4(3)
            gate_block(2, 0)
            gate_block(2, 1)
            wo = [next_group("wo"), next_group("wo", hold=1)]

            def h2_stage2(t):
                norm_stage2(F4(HFB[t]), rs(HFB[t], 4), lambda half, t=t: ht_dst3(half, t),
                            lambda half: rs(HT + half * 4, 4))

            for t in range(4):
                for nh in range(2):
                    idx, wv, wres = wo[nh]
                    b = psC.next()
                    for kc in range(8):
                        mm(ps[b][:, :], Bv(MRGT + kc, t * 128, (t + 1) * 128), wv(kc, 0, 512), kc == 0, kc == 7,
                           wres + rs(MRGT + kc), [RP[b]])
                    xt = F4(XIN[s][t], nh * 512, (nh + 1) * 512)
                    tt("dve", xt, ps[b][:, :], xt, ALU.add, [RP[b]] + rs(XIN[s][t], 4), rs(XIN[s][t], 4))
                norm_stage1(F4(XIN[s][t]), rs(XIN[s][t], 4), F4(HFB[t]), rs(HFB[t], 4), gmlp[:], R_gmlp)
                if t >= 2:
                    h2_stage2(t - 2)
            h2_stage2(2)
            h2_stage2(3)
            for g8 in range(8):
                idx, wv, wres = next_group("w1")
                for j in range(4):
                    fc = g8 * 4 + j
                    b = proj(wv, wres, j, HT)
                    r_ = GT[j]
                    act(Fv(r_), ps[b][:, :], AF.Relu, [RP[b]], rs(r_, 2))
                    tt("pool", Bv(AT + fc), Fv(r_), Fv(r_), ALU.mult, rs(r_, 2), rs(AT + fc))
            pre = (T + 1 < nchunks) and ("D" not in SKIP)
            s2 = (T + 1) % 2
            if pre:
                for t in range(4):
                    norm_stage1(F4(XIN[s2][t]), rs(XIN[s2][t], 4), F4(HFB[t]), rs(HFB[t], 4), gmix[:], R_gmix)
            for nh in range(2):
                accb = [psC.next() for _ in range(4)]
                for fb in range(4):
                    idx, wv, wres = next_group("w2")
                    for t in range(4):
                        for kc in range(8):
                            fc = fb * 8 + kc
                            mm(ps[accb[t]][:, :], Bv(AT + fc, t * 128, (t + 1) * 128), wv(kc, 0, 512),
                               fb == 0 and kc == 0, fb == 3 and kc == 7, wres + rs(AT + fc), [RP[accb[t]]])
                    if pre and nh == 1:
                        norm_stage2(F4(HFB[fb]), rs(HFB[fb], 4), lambda half, t=fb: ht_dst3(half, t),
                                    lambda half: rs(HT + half * 4, 4))
                for t in range(4):
                    xt = F4(XIN[s][t], nh * 512, (nh + 1) * 512)
                    tt("dve", xt, ps[accb[t]][:, :], xt, ALU.add, [RP[accb[t]]] + rs(XIN[s][t], 4), rs(XIN[s][t], 4))
                    if nh == 1:
                        out_dmas.append(dma("sp", out[T * 512 + t * 128:T * 512 + (t + 1) * 128, :], F4(XIN[s][t]),
                                            reads=rs(XIN[s][t], 4)))

        tail = [o for q in P.dlast for o in P.dlast[q] if o is not None]
        P.op("sp", None, deps=tail + out_dmas)
        P.finalize()
        build.stats = P.stats()
        build.P = P
        with nc.Block() as block:
            P.emit(block)
    return nc


def make_consts():
    c = np.zeros((128, 768), np.float32)
    c[:, 0:128] = np.eye(128, dtype=np.float32)
    k = np.arange(128)[:, None]
    q = np.arange(128)[None, :]
    c[:, 128:256] = (q >= k).astype(np.float32)
    c[:, 256:384] = ((k // 64) == (q // 64)).astype(np.float32)
    c[:, 384:512] = 1.0
    c[:, 512:640] = np.eye(128, dtype=np.float32)
    c[:, 640:768] = np.where(k > q, -30000.0, 0.0).astype(np.float32)
    return c


_NC_CACHE = {}


def kernel(**inputs):
    n = 8
    if "nc" not in _NC_CACHE:
        _NC_CACHE["nc"] = build()
    nc = _NC_CACHE["nc"]
    consts = make_consts()
    shared = {"consts": consts}
    for k, v in inputs.items():
        if k in ("x", "mem"):
            continue
        a = np.ascontiguousarray(np.asarray(v, dtype=np.float32))
        shared[k] = a.reshape(a.shape[1:])
    x = np.asarray(inputs["x"], dtype=np.float32)
    mem = np.asarray(inputs["mem"], dtype=np.float32)
    in_maps = []
    for b in range(n):
        m = dict(shared)
        m["x"] = np.ascontiguousarray(x[b])
        m["mem"] = np.ascontiguousarray(mem[b])
        in_maps.append(m)
    res = run_bass_kernel_spmd(nc, in_maps, core_ids=list(range(n)))
    return np.stack([np.asarray(r["out"], dtype=np.float32) for r in res.results], axis=0)
```

```python
import numpy as np
from contextlib import ExitStack
import concourse.bass as bass
import concourse.mybir as mybir
from concourse.bass_utils import run_bass_kernel_spmd

F32 = mybir.dt.float32
BF16 = mybir.dt.bfloat16
AF = mybir.ActivationFunctionType
ALU = mybir.AluOpType
AX = mybir.AxisListType

ENGS = ["pe", "act", "dve", "pool", "sp"]
BLOCKNAME = {"pe": "tensor", "act": "scalar", "dve": "vector", "pool": "gpsimd", "sp": "sync"}
N_DMA_SEMS = {"sp": 32, "pool": 16}

S = 4096
D = 1024
NSLOT = 186
EPS = 1e-6


class Res:
    __slots__ = ("name", "w", "r")

    def __init__(self, name=""):
        self.name = name
        self.w = None
        self.r = []


class Op:
    __slots__ = ("eng", "fn", "deps", "pos", "is_dma", "dsem", "dval", "mark", "markidx",
                 "clock", "waits", "label")


class Prog:
    def __init__(self, nc, es):
        self.nc = nc
        self.ops = {e: [] for e in ENGS}
        self.all = []
        self.esem = {e: es.enter_context(nc.semaphore("s_" + e)) for e in ENGS}
        self.dsems = {q: [es.enter_context(nc.semaphore("d_%s_%d" % (q, i))) for i in range(n)]
                      for q, n in N_DMA_SEMS.items()}
        self.dcnt = {q: [0] * n for q, n in N_DMA_SEMS.items()}
        self.dlast = {q: [None] * n for q, n in N_DMA_SEMS.items()}
        self.drr = {q: 0 for q in N_DMA_SEMS}
        self.label = ""

    def op(self, eng, fn, reads=(), writes=(), deps=(), dma=False):
        o = Op()
        o.eng = eng
        o.fn = fn
        o.is_dma = dma
        o.mark = False
        o.markidx = None
        o.dsem = None
        o.dval = None
        o.label = self.label
        d = {}
        for x in deps:
            if x is not None:
                d[x] = True
        for r in reads:
            if r.w is not None:
                d[r.w] = True
        for w in writes:
            if w.w is not None:
                d[w.w] = True
            for x in w.r:
                if x not in d:
                    d[x] = False
        for r in reads:
            r.r.append(o)
        for w in writes:
            w.w = o
            w.r = []
        if dma:
            q = eng
            i = self.drr[q]
            self.drr[q] = (i + 1) % len(self.dsems[q])
            prev = self.dlast[q][i]
            if prev is not None:
                d[prev] = True
            self.dcnt[q][i] += 1
            o.dsem = self.dsems[q][i]
            o.dval = 16 * self.dcnt[q][i]
            self.dlast[q][i] = o
        d.pop(o, None)
        o.deps = d
        o.pos = len(self.ops[eng])
        self.ops[eng].append(o)
        self.all.append(o)
        return o

    def finalize(self):
        ck = {e: {f: -1 for f in ENGS} for e in ENGS}
        kd = {e: set() for e in ENGS}
        for o in self.all:
            e = o.eng
            c = ck[e]
            best = {}
            dwaits = []
            for d, strong in o.deps.items():
                if d.is_dma:
                    if d not in kd[e]:
                        kd[e].add(d)
                        dwaits.append(d)
                    continue
                if d.eng == e:
                    if e in ("pe", "sp"):
                        continue
                if c[d.eng] >= d.pos:
                    continue
                b = best.get(d.eng)
                if b is None or d.pos > b.pos:
                    best[d.eng] = d
            waits = []
            for d in dwaits:
                for f in ENGS:
                    if d.clock[f] > c[f]:
                        c[f] = d.clock[f]
                waits.append(d)
            for f, d in best.items():
                if c[f] >= d.pos:
                    continue
                d.mark = True
                for g in ENGS:
                    if d.clock[g] > c[g]:
                        c[g] = d.clock[g]
                if d.pos > c[f]:
                    c[f] = d.pos
                waits.append(d)
            o.waits = waits
            o.clock = dict(c)
        self.nmarks = {}
        for e in ENGS:
            n = 0
            for o in self.ops[e]:
                if o.mark:
                    n += 1
                    o.markidx = n
            self.nmarks[e] = n

    def emit(self, block):
        for e in ENGS:
            if not self.ops[e]:
                continue

            def body(eng, e=e):
                for o in self.ops[e]:
                    for d in o.waits:
                        if d.is_dma:
                            eng.wait_ge(d.dsem, d.dval)
                        else:
                            eng.wait_ge(self.esem[d.eng], d.markidx)
                    if o.fn is None:
                        continue
                    ins = o.fn(eng)
                    if o.is_dma:
                        ins.then_inc(o.dsem, 16)
                    elif o.mark:
                        ins.then_inc(self.esem[e], 1)

            getattr(block, BLOCKNAME[e])(body)

    def stats(self):
        return {e: (len(self.ops[e]), self.nmarks.get(e)) for e in ENGS}


import os
SKIP = set(os.environ.get("MK_SKIP", "").split(","))


def build(nheads=8, nqc=8, nchunks=8, dbg=False):
    nc = bass.Bass("TRN2", target_bir_lowering=False)

    def din(name, shape):
        return nc.dram_tensor(name, list(shape), F32, kind="ExternalInput").ap()

    x = din("x", [S, D])
    mem = din("mem", [256, D])
    norm_mix_g = din("norm_mix_g", [D])
    norm_mem_g = din("norm_mem_g", [D])
    w_in = din("w_in", [D, 10240])
    b_gate = din("b_gate", [3072])
    q_norm_g = din("q_norm_g", [64])
    k_norm_g = din("k_norm_g", [64])
    lam_q1 = din("lam_q1", [64])
    lam_k1 = din("lam_k1", [64])
    lam_q2 = din("lam_q2", [64])
    lam_k2 = din("lam_k2", [64])
    subln_g = din("subln_g", [128])
    w_attn_o = din("w_attn_o", [D, D])
    conv_w = din("conv_w", [3, D])
    w_conv_o = din("w_conv_o", [D, D])
    w_mem_kv = din("w_mem_kv", [D, 2048])
    mq_norm_g = din("mq_norm_g", [256])
    mk_norm_g = din("mk_norm_g", [256])
    w_mem_o = din("w_mem_o", [D, D])
    w_o = din("w_o", [D, D])
    norm_mlp_g = din("norm_mlp_g", [D])
    w_mlp_in = din("w_mlp_in", [D, 4096])
    w_mlp_out = din("w_mlp_out", [4096, D])
    consts = din("consts", [128, 768])
    out = nc.dram_tensor("out", [S, D], F32, kind="ExternalOutput").ap()

    okind = "ExternalOutput" if dbg else "Internal"
    sc_in = nc.dram_tensor("sc_in", [D, 7168], BF16, kind="Internal").ap()
    sc_sq = [nc.dram_tensor("sc_sq%d" % i, [D, D], BF16, kind="Internal").ap() for i in range(4)]
    sc_w1 = nc.dram_tensor("sc_w1", [D, 4096], BF16, kind="Internal").ap()
    sc_w2 = nc.dram_tensor("sc_w2", [4096, D], BF16, kind="Internal").ap()
    oTd = nc.dram_tensor("oTd", [D, S], BF16, kind=okind).ap()

    es = ExitStack()
    with es:
        P = Prog(nc, es)

        def sbt(name, shape, dt):
            return es.enter_context(nc.sbuf_tensor(name, shape, dt))

        AB = sbt("arena", [128, NSLOT * 512], BF16)
        RS = [Res("s%d" % i) for i in range(NSLOT)]
        identF = sbt("identF", [128, 128], F32)
        cb = sbt("cb", [128, 640], BF16)
        vecs = sbt("vecs", [128, 64], F32)
        sc = sbt("sc", [128, 64], F32)
        gmix = sbt("gmix", [128, D], F32)
        gmlp = sbt("gmlp", [128, D], F32)
        kmT = sbt("kmT", [128, 8 * 256], BF16)
        vm = sbt("vm", [128, 2 * D], BF16)
        halo = sbt("halo", [128, 16], F32)
        rsd = sbt("rsd", [128, 64], F32)
        epi = sbt("epi", [128, 32], F32)
        gsubb = sbt("gsubb", [128, 128], F32)
        R_epi = [Res("epi%d" % i) for i in range(4)]
        R_gsubb = Res("gsubb")
        R_ident, R_cb, R_vecs, R_sc, R_gmix, R_gmlp, R_kmT, R_vm = (Res(n) for n in
            ["ident", "cb", "vecs", "sc", "gmix", "gmlp", "kmT", "vm"])
        R_halo = [Res("halo%d" % i) for i in range(8)]
        R_rsd = [Res("rsd%d" % i) for i in range(64)]
        R_scc = [Res("scc%d" % i) for i in range(64)]
        psall = es.enter_context(nc.psum_tensor("psall", [128, 4096], F32))
        ps = [psall[:, i * 512:(i + 1) * 512] for i in range(8)]
        RP = [Res("ps%d" % i) for i in range(8)]

        tri = cb[:, 0:128]
        blk = cb[:, 128:256]
        ones = cb[:, 256:384]
        identB = cb[:, 384:512]
        negtri = cb[:, 512:640]

        def Bv(i, a=0, b=512):
            return AB[:, i * 512 + a:i * 512 + b]

        def Fv(i, a=0, b=512):
            return AB[:, i * 512:(i + 2) * 512].bitcast(F32)[:, a:b]

        def F4(i, a=0, b=1024):
            return AB[:, i * 512:(i + 4) * 512].bitcast(F32)[:, a:b]

        def rs(i, n=1):
            return RS[i:i + n]

        def dma(q, out_ap, in_ap, reads=(), writes=(), deps=()):
            return P.op(q, lambda e: e.dma_start(out=out_ap, in_=in_ap), reads, writes, deps, dma=True)

        def mm(out_ap, lhsT, rhs, start, stop, reads, writes, sgc=False):
            return P.op("pe", lambda e: e.matmul(out_ap, lhsT=lhsT, rhs=rhs, start=start, stop=stop,
                                                 skip_group_check=sgc), reads, writes)

        def tr(out_ap, in_ap, ident_ap, reads, writes):
            return P.op("pe", lambda e: e.transpose(out=out_ap, in_=in_ap, identity=ident_ap), reads, writes)

        def act(out_ap, in_ap, func, reads, writes, bias=None, scale=None, accum=None):
            kw = {}
            if bias is not None:
                kw["bias"] = bias
            if scale is not None:
                kw["scale"] = scale
            if accum is not None:
                kw["accum_out"] = accum
            return P.op("act", lambda e: e.activation(out=out_ap, in_=in_ap, func=func, **kw), reads, writes)

        def tcopy(eng, out_ap, in_ap, reads, writes):
            return P.op(eng, lambda e: e.tensor_copy(out=out_ap, in_=in_ap), reads, writes)

        def tt(eng, out_ap, a, b, op, reads, writes):
            return P.op(eng, lambda e: e.tensor_tensor(out=out_ap, in0=a, in1=b, op=op), reads, writes)

        def tsc(eng, out_ap, a, s1, op0, reads, writes, s2=None, op1=None):
            if op1 is None:
                return P.op(eng, lambda e: e.tensor_scalar(out=out_ap, in0=a, scalar1=s1, scalar2=None, op0=op0),
                            reads, writes)
            return P.op(eng, lambda e: e.tensor_scalar(out=out_ap, in0=a, scalar1=s1, scalar2=s2, op0=op0, op1=op1),
                        reads, writes)

        def stt(out_ap, a, s, b, op0, op1, reads, writes):
            return P.op("dve", lambda e: e.scalar_tensor_tensor(out=out_ap, in0=a, scalar=s, in1=b, op0=op0, op1=op1),
                        reads, writes)

        def memset(eng, ap, val, writes):
            return P.op(eng, lambda e: e.memset(ap, val), (), writes)

        class Ring:
            def __init__(self, items):
                self.items = list(items)
                self.i = 0

            def next(self):
                v = self.items[self.i % len(self.items)]
                self.i += 1
                return v

        out_dmas = []

        dma("sp", identF[:], consts[:, 0:128], writes=[R_ident])
        dma("pool", cb[:], consts[:, 128:768], writes=[R_cb])
        dma("sp", gmix[:], norm_mix_g.partition_broadcast(128), writes=[R_gmix])
        dma("sp", gmlp[:], norm_mlp_g.partition_broadcast(128), writes=[R_gmlp])
        memset("dve", sc[:, 0:1], EPS, [R_scc[0]])
        dma("sp", gsubb[:], subln_g.partition_broadcast(128), writes=[R_gsubb])
        tsc("dve", gsubb[:], gsubb[:], 0.8, ALU.mult, [R_gsubb], [R_gsubb])
        memset("dve", halo[:], 0.0, R_halo)

        STG = 162
        stg_full = AB[:, STG * 512:(STG + 1) * 512].bitcast(F32)
        memset("dve", stg_full[:, 0:128], 0.0, rs(STG))

        def stg_rows(r0, r1, c0=0, c1=128):
            return AB[r0:r1, STG * 512:(STG + 1) * 512].bitcast(F32)[:, c0:c1]

        dma("sp", stg_rows(0, 24), b_gate.rearrange("(r c) -> r c", c=128), writes=rs(STG))
        dma("sp", stg_rows(24, 48), conv_w.rearrange("j (c p) -> (j c) p", p=128), writes=rs(STG))
        dma("sp", stg_rows(48, 50), mq_norm_g.rearrange("(r c) -> r c", c=128), writes=rs(STG))
        dma("sp", stg_rows(50, 52), mk_norm_g.rearrange("(r c) -> r c", c=128), writes=rs(STG))
        dma("sp", stg_rows(52, 53, 0, 64), q_norm_g.rearrange("(o n) -> o n", o=1), writes=rs(STG))
        dma("sp", stg_rows(52, 53, 64, 128), q_norm_g.rearrange("(o n) -> o n", o=1), writes=rs(STG))
        dma("sp", stg_rows(53, 54, 0, 64), k_norm_g.rearrange("(o n) -> o n", o=1), writes=rs(STG))
        dma("sp", stg_rows(53, 54, 64, 128), k_norm_g.rearrange("(o n) -> o n", o=1), writes=rs(STG))
        dma("sp", stg_rows(54, 55), subln_g.rearrange("(o n) -> o n", o=1), writes=rs(STG))
        tr(ps[0][:, 0:64], stg_rows(0, 64), identF[0:64, 0:64], rs(STG) + [R_ident], [RP[0]])
        tcopy("dve", vecs[:], ps[0][:, 0:64], [RP[0]], [R_vecs])
        tsc("dve", vecs[:, 55:56], vecs[:, 52:53], 0.125, ALU.mult, [R_vecs], [R_vecs])
        tsc("dve", vecs[:, 56:57], vecs[:, 54:55], 0.8, ALU.mult, [R_vecs], [R_vecs])
        tsc("dve", vecs[:, 57:59], vecs[:, 48:50], 1.0 / 16.0, ALU.mult, [R_vecs], [R_vecs])
        V_BG = lambda n: vecs[:, n:n + 1]
        V_CW = lambda j, c: vecs[:, 24 + j * 8 + c:25 + j * 8 + c]
        V_MKG = lambda dc: vecs[:, 50 + dc:51 + dc]
        V_MQG16 = lambda dc: vecs[:, 57 + dc:58 + dc]
        V_GK = vecs[:, 53:54]
        V_GQ8 = vecs[:, 55:56]
        V_SUB08 = vecs[:, 56:57]
        EPSC = sc[:, 0:1]
        LAM = sc[:, 1:2]
        NEGLAM = sc[:, 2:3]
        NEGM = sc[:, 3:4]
        NEGMM = sc[:, 4:5]

        BC = 156
        bc = AB[:, BC * 512:(BC + 2) * 512].bitcast(F32)
        for i, v in enumerate([lam_q1, lam_k1, lam_q2, lam_k2, q_norm_g, k_norm_g]):
            dma("sp", bc[:, i * 64:(i + 1) * 64], v.partition_broadcast(128), writes=rs(BC, 2))
        MB = 158
        mb = AB[:, MB * 512:(MB + 4) * 512].bitcast(F32)
        dma("sp", mb[:, 0:256], mq_norm_g.partition_broadcast(128), writes=rs(MB, 4))
        dma("sp", mb[:, 256:512], mk_norm_g.partition_broadcast(128), writes=rs(MB, 4))
        tt("dve", bc[:, 384:448], bc[:, 0:64], bc[:, 64:128], ALU.mult, rs(BC, 2), rs(BC, 2))
        tt("dve", bc[:, 448:512], bc[:, 128:192], bc[:, 192:256], ALU.mult, rs(BC, 2), rs(BC, 2))
        P.op("dve", lambda e: e.tensor_reduce(out=sc[:, 5:6], in_=bc[:, 384:448], axis=AX.X, op=ALU.add),
             rs(BC, 2), [R_scc[5]])
        P.op("dve", lambda e: e.tensor_reduce(out=sc[:, 6:7], in_=bc[:, 448:512], axis=AX.X, op=ALU.add),
             rs(BC, 2), [R_scc[6]])
        act(sc[:, 7:9], sc[:, 5:7], AF.Exp, [R_scc[5], R_scc[6]], [R_scc[7], R_scc[8]])
        tt("dve", sc[:, 9:10], sc[:, 7:8], sc[:, 8:9], ALU.subtract, [R_scc[7], R_scc[8]], [R_scc[9]])
        tsc("dve", sc[:, 1:2], sc[:, 9:10], 0.2, ALU.add, [R_scc[9]], [R_scc[1]])
        tsc("dve", sc[:, 2:3], sc[:, 1:2], -1.0, ALU.mult, [R_scc[1]], [R_scc[2]])
        P.op("dve", lambda e: e.tensor_reduce(out=sc[:, 10:11], in_=bc[:, 256:320], axis=AX.X, op=ALU.max,
                                              apply_absolute_value=True), rs(BC, 2), [R_scc[10]])
        P.op("dve", lambda e: e.tensor_reduce(out=sc[:, 11:12], in_=bc[:, 320:384], axis=AX.X, op=ALU.max,
                                              apply_absolute_value=True), rs(BC, 2), [R_scc[11]])
        tt("dve", sc[:, 12:13], sc[:, 10:11], sc[:, 11:12], ALU.mult, [R_scc[10], R_scc[11]], [R_scc[12]])
        tsc("dve", sc[:, 3:4], sc[:, 12:13], -8.0, ALU.mult, [R_scc[12]], [R_scc[3]])
        P.op("dve", lambda e: e.tensor_reduce(out=sc[:, 13:14], in_=mb[:, 0:256], axis=AX.X, op=ALU.max,
                                              apply_absolute_value=True), rs(MB, 4), [R_scc[13]])
        P.op("dve", lambda e: e.tensor_reduce(out=sc[:, 14:15], in_=mb[:, 256:512], axis=AX.X, op=ALU.max,
                                              apply_absolute_value=True), rs(MB, 4), [R_scc[14]])
        tt("dve", sc[:, 15:16], sc[:, 13:14], sc[:, 14:15], ALU.mult, [R_scc[13], R_scc[14]], [R_scc[15]])
        tsc("dve", sc[:, 4:5], sc[:, 15:16], -16.0, ALU.mult, [R_scc[15]], [R_scc[4]])

        rsd_ctr = [0]
        psA = Ring([0, 1, 2, 3])

        def norm_stage1(xt_ap, xt_res, h_ap, h_res, g_ap, g_res, split=False, junk=None):
            i = rsd_ctr[0] % 32
            rsd_ctr[0] += 1
            c0, c1 = 2 * i, 2 * i + 1
            if junk is None:
                act(h_ap, xt_ap, AF.Square, xt_res, h_res + [R_rsd[c0]], accum=rsd[:, c0:c0 + 1])
            else:
                act(junk[0], xt_ap, AF.Square, xt_res, junk[1] + [R_rsd[c0]], accum=rsd[:, c0:c0 + 1])
            act(rsd[:, c1:c1 + 1], rsd[:, c0:c0 + 1], AF.Ln, [R_rsd[c0], R_scc[0]], [R_rsd[c1]],
                bias=EPSC, scale=1.0 / D)
            act(rsd[:, c0:c0 + 1], rsd[:, c1:c1 + 1], AF.Exp, [R_rsd[c1]], [R_rsd[c0]], scale=-0.5)
            if split:
                act(h_ap, xt_ap, AF.Copy, xt_res + [R_rsd[c0]], h_res, scale=rsd[:, c0:c0 + 1])
                tt("pool", h_ap, h_ap, g_ap, ALU.mult, h_res + [g_res], h_res)
            else:
                stt(h_ap, xt_ap, rsd[:, c0:c0 + 1], g_ap, ALU.mult, ALU.mult,
                    xt_res + [R_rsd[c0], g_res], h_res)

        def norm_stage2(h_ap, h_res, dst3_fn, dst_res_fn):
            for half in range(2):
                b = psA.next()
                for j in range(4):
                    kc = half * 4 + j
                    tr(ps[b][:, j * 128:(j + 1) * 128], h_ap[:, kc * 128:(kc + 1) * 128], identF[:],
                       h_res + [R_ident], [RP[b]])
                if "A" in SKIP:
                    d3 = dst3_fn(half)
                    for j in range(4):
                        tcopy("dve", d3[:, j, :], ps[b][:, j * 128:(j + 1) * 128], [RP[b]], dst_res_fn(half))
                else:
                    tcopy("dve", dst3_fn(half), ps[b][:, :].rearrange("p (k n) -> p k n", n=128), [RP[b]],
                          dst_res_fn(half))

        def norm_transpose(xt_ap, xt_res, h_ap, h_res, g_ap, g_res, dst3_fn, dst_res_fn, split=False, junk=None):
            norm_stage1(xt_ap, xt_res, h_ap, h_res, g_ap, g_res, split, junk)
            norm_stage2(h_ap, h_res, dst3_fn, dst_res_fn)

        MX = [100, 104]
        GM = 108
        MH = [112, 116]
        MN = 120
        WKV = 124
        dma("sp", F4(GM), norm_mem_g.partition_broadcast(128), writes=rs(GM, 4))
        dma("pool", AB[:, WKV * 512:(WKV + 32) * 512].rearrange("p (k n) -> p k n", n=2048),
            w_mem_kv.rearrange("(k p) n -> p k n", p=128), writes=rs(WKV, 32))

        def memnT(kc, a, b):
            return AB[:, MN * 512 + kc * 256 + a:MN * 512 + kc * 256 + b]

        def wkv(kc, a, b):
            return AB[:, WKV * 512 + kc * 2048 + a:WKV * 512 + kc * 2048 + b]

        def do_mem_kv():
            for mt in range(2 if "memkv" not in SKIP else 0):
                dma("sp", F4(MX[mt]), mem[mt * 128:(mt + 1) * 128, :], writes=rs(MX[mt], 4))
                norm_transpose(F4(MX[mt]), rs(MX[mt], 4), F4(MH[mt]), rs(MH[mt], 4), F4(GM), RS[GM],
                               lambda half, mt=mt: AB[:, MN * 512 + half * 1024:MN * 512 + (half + 1) * 1024]
                               .rearrange("p (k n) -> p k n", n=256)[:, :, mt * 128:(mt + 1) * 128],
                               lambda half: rs(MN, 4))
            KT0 = 164
            for hd in range(4 if ("memkv" not in SKIP and "memk" not in SKIP) else 0):
                for dc in range(2):
                    c = hd * 2 + dc
                    b = psA.next()
                    for kc in range(8):
                        mm(ps[b][:, 0:256], wkv(kc, c * 128, (c + 1) * 128), memnT(kc, 0, 256), kc == 0, kc == 7,
                           rs(WKV, 32) + rs(MN, 4), [RP[b]])
                    tcopy("dve", Fv(164 + 2 * dc, 0, 256), ps[b][:, 0:256], [RP[b]], rs(164 + 2 * dc, 2))
                    act(Bv(168 + dc, 0, 256), Fv(164 + 2 * dc, 0, 256), AF.Square, rs(164 + 2 * dc, 2), rs(168 + dc))
                if "K2" in SKIP:
                    continue
                b = psA.next()
                for dc in range(2):
                    mm(ps[b][:, 0:256], ones, Bv(168 + dc, 0, 256), dc == 0, dc == 1, [R_cb] + rs(168 + dc), [RP[b]])
                if "K3" in SKIP:
                    continue
                act(Fv(170, 0, 256), ps[b][:, 0:256], AF.Ln, [RP[b], R_scc[0]], rs(170, 2), bias=EPSC, scale=1.0 / 256)
                act(Fv(172, 0, 256), Fv(170, 0, 256), AF.Exp, rs(170, 2), rs(172, 2), scale=-0.5)
                if "K4" in SKIP:
                    continue
                for dc in range(2):
                    c = hd * 2 + dc
                    stt(kmT[:, c * 256:(c + 1) * 256], Fv(164 + 2 * dc, 0, 256), V_MKG(dc), Fv(172, 0, 256),
                        ALU.mult, ALU.mult, rs(164 + 2 * dc, 2) + rs(172, 2) + [R_vecs], [R_kmT])
            for mt in range(2 if ("memkv" not in SKIP and "memv" not in SKIP) else 0):
                for nh in range(2):
                    b = psA.next()
                    for kc in range(8):
                        mm(ps[b][:, :], memnT(kc, mt * 128, (mt + 1) * 128), wkv(kc, 1024 + nh * 512, 1024 + (nh + 1) * 512),
                           kc == 0, kc == 7, rs(WKV, 32) + rs(MN, 4), [RP[b]])
                    tcopy("dve", vm[:, mt * 1024 + nh * 512:mt * 1024 + (nh + 1) * 512], ps[b][:, :], [RP[b]], [R_vm])


        R_blk = {}
        cast_jobs = []
        for j in range(7):
            R_blk[("in", j)] = Res()
            cast_jobs.append((sc_in[:, j * 1024:(j + 1) * 1024], w_in[:, 3072 + j * 1024:3072 + (j + 1) * 1024],
                              R_blk[("in", j)]))
        for i, wsrc in enumerate([w_attn_o, w_conv_o, w_mem_o, w_o]):
            R_blk[("sq", i)] = Res()
            cast_jobs.append((sc_sq[i], wsrc, R_blk[("sq", i)]))
        for j in range(4):
            R_blk[("w1", j)] = Res()
            cast_jobs.append((sc_w1[:, j * 1024:(j + 1) * 1024], w_mlp_in[:, j * 1024:(j + 1) * 1024], R_blk[("w1", j)]))
        for j in range(4):
            R_blk[("w2", j)] = Res()
            cast_jobs.append((sc_w2[j * 1024:(j + 1) * 1024, :], w_mlp_out[j * 1024:(j + 1) * 1024, :], R_blk[("w2", j)]))
        cast_jobs_small = []
        R_pieces = {}
        for (o_ap, i_ap, r) in cast_jobs:
            R_pieces[id(r)] = [Res() for _ in range(8)]
            for rr in range(8):
                cast_jobs_small.append((o_ap[rr * 128:(rr + 1) * 128, :], i_ap[rr * 128:(rr + 1) * 128, :],
                                        R_pieces[id(r)][rr]))
        cast_jobs = cast_jobs_small
        cast_ptr = [0]
        cast_ops = []

        def issue_casts(n):
            if "casts" in SKIP:
                return
            for _ in range(n):
                if cast_ptr[0] < len(cast_jobs):
                    o_ap, i_ap, r = cast_jobs[cast_ptr[0]]
                    cast_ptr[0] += 1
                    lag = int(os.environ.get("MK_CASTLAG", "4"))
                    dep = cast_ops[-lag] if len(cast_ops) >= lag else None
                    cast_ops.append(dma("pool", o_ap, i_ap, writes=[r], deps=[dep]))

        def hT(kc, a, b):
            return AB[:, kc * 4096 + a:kc * 4096 + b]

        XS = [64, 68, 72, 96]
        HS = [76, 80, 84, 88]
        JUNK = 92
        for t in range(32 if "phaseA" not in SKIP else 0):
            xs_ = XS[t % 4]
            hs_ = HS[t % 4]
            dma("sp", F4(xs_), x[t * 128:(t + 1) * 128, :], writes=rs(xs_, 4))
            norm_transpose(F4(xs_), rs(xs_, 4), F4(hs_), rs(hs_, 4), gmix[:], R_gmix,
                           lambda half, t=t: AB[:, half * 4 * 4096:(half * 4 + 4) * 4096]
                           .rearrange("p (k n) -> p k n", n=4096)[:, :, t * 128:(t + 1) * 128],
                           lambda half, t=t: [RS[kc * 8 + t // 4] for kc in range(half * 4, half * 4 + 4)],
                           split=(t % 2 == 1), junk=(F4(JUNK), rs(JUNK, 4)))
        do_mem_kv()

        QS = [64, 64]
        KS1 = 72
        VS = [96, 96]
        WH = [112, 118]
        PTS = list(range(124, 132))
        SQ = [132, 133, 137]
        OSQ = 134
        OOUT = [135, 136]
        QRAW = [138, 140, 166]
        LNT = [142, 144, 168]
        RSTD = [146, 148, 170]
        VAS = [96, 172]
        ACCS = [150, 152, 154]
        O0S, T1S, ONB = 156, 158, 160
        for VA_ in VAS:
            memset("pool", AB[:, VA_ * 512:VA_ * 512 + 32 * 132].rearrange("p (t c) -> p t c", c=132)[:, :, 128:132],
                   1.0, rs(VA_, 9))

        def wh(par, j, kc):
            base = WH[par] * 512 + j * 1024 + kc * 128
            return AB[:, base:base + 128]

        def load_wh(h):
            par = h % 2
            for j in range(3):
                base = WH[par] * 512 + j * 1024
                dma("pool", AB[:, base:base + 1024].rearrange("p (k n) -> p k n", n=128),
                    w_in[:, j * 1024 + h * 128:j * 1024 + (h + 1) * 128].rearrange("(k p) n -> p k n", p=128),
                    writes=rs(WH[par] + 2 * j, 2))

        if nheads > 0:
            load_wh(0)
        psS = Ring([0, 1, 2, 3])
        pt_ring = Ring(PTS[0::2])
        spair = Ring([0, 2])
        deferred = []

        def run_deferred(n=1000):
            while deferred and n > 0:
                deferred.pop(0)()
                n -= 1
        nrm_i = [0]
        oout_i = [0]
        for h in range(nheads):
            par = h % 2
            pend = []

            def qk_stat_a(info):
                (b, which, c, i) = info
                P.label = "stat w%d c%d" % (which, c)
                b2 = psS.next()
                mm(ps[b2][:, :], blk, Bv(SQ[i]), True, True, [R_cb] + rs(SQ[i]), [RP[b2]])
                act(Fv(LNT[i]), ps[b2][:, :], AF.Ln, [RP[b2], R_scc[0]], rs(LNT[i], 2), bias=EPSC, scale=1.0 / 64)
                act(Fv(RSTD[i]), Fv(LNT[i]), AF.Exp, rs(LNT[i], 2), rs(RSTD[i], 2), scale=-0.5)

            def qk_stat_b(info):
                (b, which, c, i) = info
                if which == 0:
                    dst = QS[par] + c
                    stt(Bv(dst), Fv(QRAW[i]), V_GQ8, Fv(RSTD[i]), ALU.mult, ALU.mult,
                        rs(QRAW[i], 2) + rs(RSTD[i], 2) + [R_vecs], rs(dst))
                else:
                    dst = KS1 + c
                    stt(Bv(dst), Fv(QRAW[i]), V_GK, Fv(RSTD[i]), ALU.mult, ALU.mult,
                        rs(QRAW[i], 2) + rs(RSTD[i], 2) + [R_vecs], rs(dst))

            for which in range(2):
                for c in range(8):
                    i = nrm_i[0] % 3
                    nrm_i[0] += 1
                    old_info = pend.pop(0) if len(pend) >= 2 else None
                    if old_info is not None:
                        qk_stat_a(old_info)
                    b = psS.next()
                    P.label = "proj h%d w%d c%d" % (h, which, c)
                    for kc in range(8):
                        mm(ps[b][:, :], wh(par, which, kc), hT(kc, c * 512, (c + 1) * 512), kc == 0, kc == 7,
                           rs(WH[par], 6) + rs(kc * 8 + c), [RP[b]])
                    tcopy("dve", Fv(QRAW[i]), ps[b][:, :], [RP[b]], rs(QRAW[i], 2))
                    if c % 2 == 0:
                        tt("pool", Bv(SQ[i]), Fv(QRAW[i]), Fv(QRAW[i]), ALU.mult, rs(QRAW[i], 2), rs(SQ[i]))
                    else:
                        act(Bv(SQ[i]), Fv(QRAW[i]), AF.Square, rs(QRAW[i], 2), rs(SQ[i]))
                    if old_info is not None:
                        qk_stat_b(old_info)
                    pend.append((b, which, c, i))
                    if c in (0, 2, 4, 6):
                        run_deferred(1)
            while pend:
                info = pend.pop(0)
                qk_stat_a(info)
                qk_stat_b(info)
            VA = VAS[par]

            def vproj_unit(hh, g, j, b):
                pp = hh % 2
                t = g * 4 + j
                for kc in range(8):
                    mm(ps[b][:, j * 128:(j + 1) * 128], hT(kc, t * 128, (t + 1) * 128), wh(pp, 2, kc),
                       kc == 0, kc == 7, rs(WH[pp], 6) + rs(kc * 8 + g), [RP[b]])
                if j == 3:
                    va = VAS[pp]
                    tcopy("dve", AB[:, va * 512 + g * 528:va * 512 + (g + 1) * 528].rearrange(
                        "p (t c) -> p t c", c=132)[:, :, 0:128],
                        ps[b][:, :].rearrange("p (t c) -> p t c", c=128), [RP[b]], rs(va, 9))

            def vproj_half(hh, g, j, hf, b):
                pp = hh % 2
                t = g * 4 + j
                for kc in range(hf * 4, hf * 4 + 4):
                    mm(ps[b][:, j * 128:(j + 1) * 128], hT(kc, t * 128, (t + 1) * 128), wh(pp, 2, kc),
                       kc == 0, kc == 7, rs(WH[pp], 6) + rs(kc * 8 + g), [RP[b]])
                if j == 3 and hf == 1:
                    va = VAS[pp]
                    tcopy("dve", AB[:, va * 512 + g * 528:va * 512 + (g + 1) * 528].rearrange(
                        "p (t c) -> p t c", c=132)[:, :, 0:128],
                        ps[b][:, :].rearrange("p (t c) -> p t c", c=128), [RP[b]], rs(va, 9))

            if h == 0:
                for g in range(8):
                    b = psS.next()
                    for j in range(4):
                        vproj_unit(0, g, j, b)
            if h + 1 < nheads:
                load_wh(h + 1)
                vunits = [(g, j, hf) for g in range(8) for j in range(4) for hf in range(2)]
            else:
                vunits = []
            ktctr = [0]
            def qk(qc, kt):
                qslot = QS[par] + qc
                j = kt - 4 * qc
                off = max(0, j) * 128
                ko = (kt % 4) * 128
                sb_ = []
                pair = spair.next()
                kslot = KS1 + kt // 4
                for sub in range(2):
                    b = pair + sub
                    p0, p1 = sub * 64, (sub + 1) * 64
                    mm(ps[b][:, off:512], AB[p0:p1, kslot * 512 + ko:kslot * 512 + ko + 128],
                       AB[p0:p1, qslot * 512 + off:qslot * 512 + 512], True, j < 0,
                       rs(kslot) + rs(qslot), [RP[b]])
                    sb_.append(b)
                if j >= 0:
                    for sub in range(2):
                        b = pair + sub
                        mm(ps[b][:, off:off + 128], identB, negtri, False, True, [R_cb], [RP[b]])
                return sb_

            def softmax(qc, kt, sb_):
                j = kt - 4 * qc
                off = max(0, j) * 128
                p0 = pt_ring.next()
                pts = [p0, p0 + 1]
                b0 = sb_[0]
                in3 = psall[:, b0 * 512:(b0 + 2) * 512].rearrange("p (s n) -> p s n", n=512)[:, :, off:512]
                out3 = AB[:, p0 * 512:(p0 + 2) * 512].rearrange("p (s n) -> p s n", n=512)[:, :, off:512]
                act(out3, in3, AF.Exp, [RP[b0], RP[b0 + 1], R_scc[3]], rs(p0, 2), bias=NEGM, scale=1.0)
                return pts

            def pv(qc, kt, pts, first_step, last_step):
                j = kt - 4 * qc
                q0 = max(0, j)
                vcol = VA * 512 + kt * 132
                for sub in range(2):
                    for qt in range(q0, 4):
                        a_ = sub * 4 + qt
                        bank = 4 + a_ // 3
                        col = (a_ % 3) * 129
                        mm(ps[bank][:, col:col + 129], Bv(pts[sub], qt * 128, (qt + 1) * 128),
                           AB[:, vcol:vcol + 129], first_step and a_ % 3 == 0, last_step,
                           rs(VA, 9) + rs(pts[sub]), [RP[bank]], sgc=True)

            steps = []
            for qc in range(nqc):
                fulls = list(range(4 * qc))
                diags = list(range(4 * qc, 4 * qc + 4))
                order = []
                if fulls:
                    per = max(1, len(fulls) // 4)
                    di = 0
                    for fi, kt_ in enumerate(fulls):
                        order.append(kt_)
                        if (fi + 1) % per == 0 and di < 4:
                            order.append(diags[di])
                            di += 1
                    order.extend(diags[di:])
                else:
                    order = diags
                for oi, kt_ in enumerate(order):
                    steps.append((qc, kt_, oi, len(order)))
            cur_pts = softmax(*steps[0][:2], qk(*steps[0][:2])) if steps else None
            for si, (qc, kt, oi, on_) in enumerate(steps):
                nkt = 4 * qc + 4
                if oi == 0:
                    issue_casts(3)
                nxt = steps[si + 1][:2] if si + 1 < len(steps) else None
                sb_n = qk(*nxt) if nxt is not None else None
                pv(qc, kt, cur_pts, oi == 0, oi == on_ - 1)
                last = oi == on_ - 1
                if last:
                    epilogue_now = True
                else:
                    epilogue_now = False
                if epilogue_now:
                    pass
                if not last:
                    if nxt is not None:
                        cur_pts = softmax(*nxt, sb_n)
                    if oi in (0, 2, 4, 6, 8):
                        run_deferred(1)
                    ktctr[0] += 1
                    if vunits and ktctr[0] % 2 == 1:
                        g_, j_, half_ = vunits.pop(0)
                        vproj_half(h + 1, g_, j_, half_, 7)
                    continue
                run_deferred()
                for bk in range(3):
                    n_ = 387 if bk < 2 else 258
                    tcopy("dve", Fv(ACCS[bk], 0, n_), ps[4 + bk][:, 0:n_], [RP[4 + bk]], rs(ACCS[bk], 2))

                def accv(a_, c0, c1):
                    return Fv(ACCS[a_ // 3], (a_ % 3) * 129 + c0, (a_ % 3) * 129 + c1)

                def epA():
                    for bk in range(3):
                        n3 = 3 if bk < 2 else 2
                        src = Fv(ACCS[bk], 0, n3 * 129).rearrange("p (a c) -> p a c", c=129)[:, :, 128:129]
                        P.op("dve", lambda e, o_=epi[:, 3 * bk:3 * bk + n3], i_=src: e.reciprocal(out=o_, in_=i_),
                             rs(ACCS[bk], 2), [R_epi[0]])
                    for qt in range(4):
                        tsc("dve", Fv(O0S, qt * 128, (qt + 1) * 128), accv(qt, 0, 128), epi[:, qt:qt + 1], ALU.mult,
                            rs(ACCS[qt // 3], 2) + [R_epi[0]], rs(O0S, 2))
                        tsc("dve", Fv(T1S, qt * 128, (qt + 1) * 128), accv(4 + qt, 0, 128), epi[:, 4 + qt:5 + qt],
                            ALU.mult, rs(ACCS[(4 + qt) // 3], 2) + [R_epi[0]], rs(T1S, 2))

                def epB():
                    stt(Fv(O0S), Fv(T1S), NEGLAM, Fv(O0S), ALU.mult, ALU.add,
                        rs(T1S, 2) + rs(O0S, 2) + [R_scc[2]], rs(O0S, 2))
                    tt("dve", Fv(T1S), Fv(O0S), Fv(O0S), ALU.mult, rs(O0S, 2), rs(T1S, 2))
                    P.op("dve", lambda e: e.tensor_reduce(out=epi[:, 8:12],
                                                          in_=Fv(T1S).rearrange("p (a c) -> p a c", c=128),
                                                          axis=AX.X, op=ALU.add), rs(T1S, 2), [R_epi[1]])

                def epC():
                    act(epi[:, 12:16], epi[:, 8:12], AF.Ln, [R_epi[1], R_scc[0]], [R_epi[2]], bias=EPSC, scale=1.0 / 128)
                    act(epi[:, 16:20], epi[:, 12:16], AF.Exp, [R_epi[2]], [R_epi[3]], scale=-0.5)
                    for qt in range(4):
                        stt(Fv(ONB, qt * 128, (qt + 1) * 128), Fv(O0S, qt * 128, (qt + 1) * 128),
                            epi[:, 16 + qt:17 + qt], gsubb[:], ALU.mult, ALU.mult,
                            rs(O0S, 2) + [R_epi[3], R_gsubb], rs(ONB, 2))

                R_o = Res()
                R_blk[("oT", h, qc)] = R_o

                def epD(h=h, qc=qc, R_o=R_o):
                    b2 = psS.next()
                    for qt in range(4):
                        tr(ps[b2][:, qt * 128:(qt + 1) * 128], Fv(ONB, qt * 128, (qt + 1) * 128), identF[:],
                           rs(ONB, 2) + [R_ident], [RP[b2]])
                    oo = OOUT[oout_i[0] % 2]
                    oout_i[0] += 1
                    tcopy("dve", Bv(oo), ps[b2][:, :], [RP[b2]], rs(oo))
                    dma("pool", oTd[h * 128:(h + 1) * 128, qc * 512:(qc + 1) * 512], Bv(oo), reads=rs(oo), writes=[R_o])

                deferred.extend([epA, epB, epC, epD])
                if nxt is not None:
                    cur_pts = softmax(*nxt, sb_n)
            while vunits:
                g_, j_, half_ = vunits.pop(0)
                vproj_half(h + 1, g_, j_, half_, 7)
        run_deferred()
        issue_casts(len(cast_jobs))

        XIN = [[0 + 16 * s + 4 * t for t in range(4)] for s in range(2)]
        HF = [32, 36]
        HT = 40
        OTC = [48, 56]
        CONVT = 64
        OMT = 72
        QMT = 80
        MRGT = 88
        AT = 64
        MRG = 96
        WR = [112, 120, 128, 136]
        TMP = 144
        XCS = [TMP + 2 * i for i in range(4)]
        INN = [TMP + 8 + 3 * i for i in range(4)]
        YT = [TMP + 20, TMP + 22]
        GT = [TMP + 24 + 2 * i for i in range(4)]
        MT = [TMP + 32 + 2 * i for i in range(5)]

        def inn(i, a, b):
            return AB[:, INN[i] * 512:(INN[i] + 3) * 512].bitcast(F32)[:, a:b]

        groups = []
        for T in range(nchunks):
            gl = []
            for j in range(2):
                for kind, cb0 in (("xc", 0), ("gc", 2048), ("gb", 1024)):
                    gl.append((kind, j, sc_in, 0, cb0 + j * 512, R_blk[("in", (cb0 + j * 512) // 1024)]))
            def gate_bo(br, j):
                gl.append(("gate", (br, j), sc_in, 0, 4096 + br * 1024 + j * 512, R_blk[("in", 4 + br)]))
                gl.append(("bo", (br, j), sc_sq[br], 0, j * 512, R_blk[("sq", br)]))
            gl.append(("qm", 0, sc_in, 0, 3072, R_blk[("in", 3)]))
            gate_bo(0, 0)
            gl.append(("qm", 1, sc_in, 0, 3072 + 512, R_blk[("in", 3)]))
            gate_bo(0, 1)
            for br in (1, 2):
                for j in range(2):
                    gate_bo(br, j)
            for j in range(2):
                gl.append(("wo", j, sc_sq[3], 0, j * 512, R_blk[("sq", 3)]))
            for j in range(8):
                gl.append(("w1", j, sc_w1, 0, j * 512, R_blk[("w1", j // 2)]))
            for nh in range(2):
                for fb in range(4):
                    gl.append(("w2", (nh, fb), sc_w2, fb * 1024, nh * 512, R_blk[("w2", fb)]))
            groups.extend(gl)
        gptr = [0]
        gissued = [0]

        def issue_groups(upto):
            while gissued[0] < min(upto, len(groups)):
                g = gissued[0]
                kind, idx, scr, r0, c0, rb = groups[g]
                slot = WR[g % 4]
                dma("sp", AB[:, slot * 512:(slot + 8) * 512].rearrange("p (k n) -> p k n", n=512),
                    scr[r0:r0 + 1024, c0:c0 + 512].rearrange("(k p) n -> p k n", p=128),
                    reads=R_pieces[id(rb)], writes=rs(slot, 8))
                gissued[0] += 1

        def next_group(kind_expect, hold=0):
            g = gptr[0]
            gptr[0] += 1
            issue_groups(g + 4 - hold)
            kind, idx, scr, r0, c0, rb = groups[g]
            assert kind == kind_expect, (kind, kind_expect)
            slot = WR[g % 4]

            def wv(kc, a, b):
                return AB[:, slot * 512 + kc * 512 + a:slot * 512 + kc * 512 + b]
            return idx, wv, rs(slot, 8)

        def load_chunk_inputs(T):
            s = T % 2
            for t in range(4):
                dma("sp", F4(XIN[s][t]), x[T * 512 + t * 128:T * 512 + (t + 1) * 128, :], writes=rs(XIN[s][t], 4))
            dma("sp", AB[:, OTC[s] * 512:(OTC[s] + 8) * 512].rearrange("p (k n) -> p k n", n=512),
                oTd[:, T * 512:(T + 1) * 512].rearrange("(k p) n -> p k n", p=128),
                reads=[R_blk[("oT", hh, T)] for hh in range(nheads) if ("oT", hh, T) in R_blk],
                writes=rs(OTC[s], 8))

        psC = Ring([0, 1, 2, 3, 4, 5, 6, 7])
        hf_i = [0]
        if nchunks > 0:
            load_chunk_inputs(0)
        for T in range(nchunks):
            s = T % 2
            if T + 1 < nchunks:
                load_chunk_inputs(T + 1)
            issue_groups(gptr[0] + 3)
            HFB = [32, 36, 144, 148]

            def ht_dst3(half, t):
                return AB[:, (HT + half * 4) * 512:(HT + half * 4 + 4) * 512].rearrange(
                    "p (k n) -> p k n", n=512)[:, :, t * 128:(t + 1) * 128]

            if T == 0 or "D" in SKIP:
                for t in range(4):
                    norm_transpose(F4(XIN[s][t]), rs(XIN[s][t], 4), F4(HFB[t]), rs(HFB[t], 4), gmix[:], R_gmix,
                                   lambda half, t=t: ht_dst3(half, t), lambda half: rs(HT + half * 4, 4))

            def proj(wv, wres, j, srcslot):
                b = psC.next()
                for kc in range(8):
                    mm(ps[b][:, :], wv(kc, j * 128, (j + 1) * 128), Bv(srcslot + kc), kc == 0, kc == 7,
                       wres + rs(srcslot + kc), [RP[b]])
                return b

            for half in range(2):
                idx, wv, wres = next_group("xc")
                for j in range(4):
                    b = proj(wv, wres, j, HT)
                    tcopy("dve", Fv(XCS[j]), ps[b][:, :], [RP[b]], rs(XCS[j], 2))
                idx, wv, wres = next_group("gc")
                for j in range(4):
                    c = half * 4 + j
                    b = proj(wv, wres, j, HT)
                    tcopy("dve", inn(j, 0, 2), halo[:, 2 * c:2 * c + 2], [R_halo[c]], rs(INN[j], 3))
                    tt("dve", inn(j, 2, 514), ps[b][:, :], Fv(XCS[j]), ALU.mult, [RP[b]] + rs(XCS[j], 2), rs(INN[j], 3))
                    tcopy("dve", halo[:, 2 * c:2 * c + 2], inn(j, 512, 514), rs(INN[j], 3), [R_halo[c]])
                idx, wv, wres = next_group("gb")
                for j in range(4):
                    c = half * 4 + j
                    y = YT[j % 2]
                    tsc("dve", Fv(y), inn(j, 0, 512), V_CW(0, c), ALU.mult, rs(INN[j], 3) + [R_vecs], rs(y, 2))
                    stt(Fv(y), inn(j, 1, 513), V_CW(1, c), Fv(y), ALU.mult, ALU.add, rs(INN[j], 3) + rs(y, 2) + [R_vecs], rs(y, 2))
                    stt(Fv(y), inn(j, 2, 514), V_CW(2, c), Fv(y), ALU.mult, ALU.add, rs(INN[j], 3) + rs(y, 2) + [R_vecs], rs(y, 2))
                    b = proj(wv, wres, j, HT)
                    tt("dve", Bv(CONVT + c), ps[b][:, :], Fv(y), ALU.mult, [RP[b]] + rs(y, 2), rs(CONVT + c))
            QRAW_M = [[144, 146], [148, 150]]
            SQ_M = [[164, 165], [166, 167]]
            LN_M = [152, 154]
            RS_M = [156, 158]
            PTM = [[160, 161], [162, 163]]
            RCP = [176, 178]

            def mS1(hd, wv, wres, hl):
                p = hd % 2
                for dc in range(2):
                    b = proj(wv, wres, hl * 2 + dc, HT)
                    tcopy("dve", Fv(QRAW_M[p][dc]), ps[b][:, :], [RP[b]], rs(QRAW_M[p][dc], 2))
                    tt("pool", Bv(SQ_M[p][dc]), Fv(QRAW_M[p][dc]), Fv(QRAW_M[p][dc]), ALU.mult,
                       rs(QRAW_M[p][dc], 2), rs(SQ_M[p][dc]))

            def mS2(hd):
                p = hd % 2
                b = psC.next()
                for dc in range(2):
                    mm(ps[b][:, :], ones, Bv(SQ_M[p][dc]), dc == 0, dc == 1, [R_cb] + rs(SQ_M[p][dc]), [RP[b]])
                act(Fv(LN_M[p]), ps[b][:, :], AF.Ln, [RP[b], R_scc[0]], rs(LN_M[p], 2), bias=EPSC, scale=1.0 / 256)
                act(Fv(RS_M[p]), Fv(LN_M[p]), AF.Exp, rs(LN_M[p], 2), rs(RS_M[p], 2), scale=-0.5)
                for dc in range(2):
                    stt(Bv(QMT + hd * 2 + dc), Fv(QRAW_M[p][dc]), V_MQG16(dc), Fv(RS_M[p]), ALU.mult, ALU.mult,
                        rs(QRAW_M[p][dc], 2) + rs(RS_M[p], 2) + [R_vecs], rs(QMT + hd * 2 + dc))

            def mS3(hd):
                p = hd % 2
                for mt in range(2):
                    b = psC.next()
                    for dc in range(2):
                        c = hd * 2 + dc
                        mm(ps[b][:, :], kmT[:, c * 256 + mt * 128:c * 256 + (mt + 1) * 128], Bv(QMT + c),
                           dc == 0, dc == 1, [R_kmT] + rs(QMT + c), [RP[b]])
                    act(Bv(PTM[p][mt]), ps[b][:, :], AF.Exp, [RP[b], R_scc[4]], rs(PTM[p][mt]), bias=NEGMM, scale=1.0)
                bl = psC.next()
                for mt in range(2):
                    mm(ps[bl][:, :], ones, Bv(PTM[p][mt]), mt == 0, mt == 1, [R_cb] + rs(PTM[p][mt]), [RP[bl]])
                tcopy("dve", Fv(RCP[p]), ps[bl][:, :], [RP[bl]], rs(RCP[p], 2))
                P.op("dve", lambda e, a=Fv(RCP[p]): e.reciprocal(out=a, in_=a), rs(RCP[p], 2), rs(RCP[p], 2))

            def mS4(hd):
                p = hd % 2
                for ec in range(2):
                    b = psC.next()
                    for mt in range(2):
                        col = mt * 1024 + hd * 256 + ec * 128
                        mm(ps[b][:, :], vm[:, col:col + 128], Bv(PTM[p][mt]), mt == 0, mt == 1,
                           [R_vm] + rs(PTM[p][mt]), [RP[b]])
                    tt("dve", Bv(OMT + hd * 2 + ec), ps[b][:, :], Fv(RCP[p]), ALU.mult,
                       [RP[b]] + rs(RCP[p], 2), rs(OMT + hd * 2 + ec))

            srcs = [OTC[s], CONVT, OMT]

            def gate_block(br, half):
                idx, wv, wres = next_group("gate")
                for j in range(4):
                    n = br * 8 + half * 4 + j
                    b = proj(wv, wres, j, HT)
                    act(Fv(GT[j]), ps[b][:, :], AF.Sigmoid, [RP[b], R_vecs], rs(GT[j], 2), bias=V_BG(n), scale=1.0)
                idx, wv, wres = next_group("bo")
                for j in range(4):
                    f = half * 4 + j
                    b = proj(wv, wres, j, srcs[br])
                    if br == 0:
                        tt("dve", Fv(MRG + 2 * f), ps[b][:, :], Fv(GT[j]), ALU.mult,
                           [RP[b]] + rs(GT[j], 2), rs(MRG + 2 * f, 2))
                    else:
                        tt("dve", Fv(GT[j]), ps[b][:, :], Fv(GT[j]), ALU.mult, [RP[b]] + rs(GT[j], 2), rs(GT[j], 2))
                        if br == 1:
                            tt("pool", Fv(MRG + 2 * f), Fv(MRG + 2 * f), Fv(GT[j]), ALU.add,
                               rs(MRG + 2 * f, 2) + rs(GT[j], 2), rs(MRG + 2 * f, 2))
                        else:
                            tt("pool", Bv(MRGT + f), Fv(MRG + 2 * f), Fv(GT[j]), ALU.add,
                               rs(MRG + 2 * f, 2) + rs(GT[j], 2), rs(MRGT + f))

            if "C" in SKIP:
                idx, wv, wres = next_group("qm")
                for hd in (0, 1):
                    mS1(hd, wv, wres, hd)
                    mS2(hd)
                    mS3(hd)
                    mS4(hd)
                gate_block(0, 0)
                idx, wv, wres = next_group("qm")
                for hd in (2, 3):
                    mS1(hd, wv, wres, hd - 2)
                    mS2(hd)
                    mS3(hd)
                    mS4(hd)
                gate_block(0, 1)
                gate_block(1, 0)
                gate_block(1, 1)
            else:
                idx, wv, wres = next_group("qm")
                mS1(0, wv, wres, 0)
                mS1(1, wv, wres, 1)
                gate_block(0, 0)
                mS2(0)
                mS2(1)
                idx, wv, wres = next_group("qm")
                mS1(2, wv, wres, 0)
                mS1(3, wv, wres, 1)
                gate_block(0, 1)
                mS3(0)
                mS3(1)
                mS2(2)
                mS2(3)
                gate_block(1, 0)
                mS4(0)
                mS4(1)
                mS3(2)
                mS3(3)
                gate_block(1, 1)
                mS4(2)
                mS4(3)
            gate_block(2, 0)
            gate_block(2, 1)
            wo = [next_group("wo"), next_group("wo", hold=1)]

            def h2_stage2(t):
                norm_stage2(F4(HFB[t]), rs(HFB[t], 4), lambda half, t=t: ht_dst3(half, t),
                            lambda half: rs(HT + half * 4, 4))

            for t in range(4):
                for nh in range(2):
                    idx, wv, wres = wo[nh]
                    b = psC.next()
                    for kc in range(8):
                        mm(ps[b][:, :], Bv(MRGT + kc, t * 128, (t + 1) * 128), wv(kc, 0, 512), kc == 0, kc == 7,
                           wres + rs(MRGT + kc), [RP[b]])
                    xt = F4(XIN[s][t], nh * 512, (nh + 1) * 512)
                    tt("dve", xt, ps[b][:, :], xt, ALU.add, [RP[b]] + rs(XIN[s][t], 4), rs(XIN[s][t], 4))
                norm_stage1(F4(XIN[s][t]), rs(XIN[s][t], 4), F4(HFB[t]), rs(HFB[t], 4), gmlp[:], R_gmlp)
                if t >= 2:
                    h2_stage2(t - 2)
            h2_stage2(2)
            h2_stage2(3)
            for g8 in range(8):
                idx, wv, wres = next_group("w1")
                for j in range(4):
                    fc = g8 * 4 + j
                    b = proj(wv, wres, j, HT)
                    r_ = GT[j]
                    act(Fv(r_), ps[b][:, :], AF.Relu, [RP[b]], rs(r_, 2))
                    tt("pool", Bv(AT + fc), Fv(r_), Fv(r_), ALU.mult, rs(r_, 2), rs(AT + fc))
            pre = (T + 1 < nchunks) and ("D" not in SKIP)
            s2 = (T + 1) % 2
            if pre:
                for t in range(4):
                    norm_stage1(F4(XIN[s2][t]), rs(XIN[s2][t], 4), F4(HFB[t]), rs(HFB[t], 4), gmix[:], R_gmix)
            for nh in range(2):
                accb = [psC.next() for _ in range(4)]
                for fb in range(4):
                    idx, wv, wres = next_group("w2")
                    for t in range(4):
                        for kc in range(8):
                            fc = fb * 8 + kc
                            mm(ps[accb[t]][:, :], Bv(AT + fc, t * 128, (t + 1) * 128), wv(kc, 0, 512),
                               fb == 0 and kc == 0, fb == 3 and kc == 7, wres + rs(AT + fc), [RP[accb[t]]])
                    if pre and nh == 1:
                        norm_stage2(F4(HFB[fb]), rs(HFB[fb], 4), lambda half, t=fb: ht_dst3(half, t),
                                    lambda half: rs(HT + half * 4, 4))
                for t in range(4):
                    xt = F4(XIN[s][t], nh * 512, (nh + 1) * 512)
                    tt("dve", xt, ps[accb[t]][:, :], xt, ALU.add, [RP[accb[t]]] + rs(XIN[s][t], 4), rs(XIN[s][t], 4))
                    if nh == 1:
                        out_dmas.append(dma("sp", out[T * 512 + t * 128:T * 512 + (t + 1) * 128, :], F4(XIN[s][t]),
                                            reads=rs(XIN[s][t], 4)))

        tail = [o for q in P.dlast for o in P.dlast[q] if o is not None]
        P.op("sp", None, deps=tail + out_dmas)
        P.finalize()
        build.stats = P.stats()
        build.P = P
        with nc.Block() as block:
            P.emit(block)
    return nc


def make_consts():
    c = np.zeros((128, 768), np.float32)
    c[:, 0:128] = np.eye(128, dtype=np.float32)
    k = np.arange(128)[:, None]
    q = np.arange(128)[None, :]
    c[:, 128:256] = (q >= k).astype(np.float32)
    c[:, 256:384] = ((k // 64) == (q // 64)).astype(np.float32)
    c[:, 384:512] = 1.0
    c[:, 512:640] = np.eye(128, dtype=np.float32)
    c[:, 640:768] = np.where(k > q, -30000.0, 0.0).astype(np.float32)
    return c


_NC_CACHE = {}


def kernel(**inputs):
    n = 8
    if "nc" not in _NC_CACHE:
        _NC_CACHE["nc"] = build()
    nc = _NC_CACHE["nc"]
    consts = make_consts()
    shared = {"consts": consts}
    for k, v in inputs.items():
        if k in ("x", "mem"):
            continue
        a = np.ascontiguousarray(np.asarray(v, dtype=np.float32))
        shared[k] = a.reshape(a.shape[1:])
    x = np.asarray(inputs["x"], dtype=np.float32)
    mem = np.asarray(inputs["mem"], dtype=np.float32)
    in_maps = []
    for b in range(n):
        m = dict(shared)
        m["x"] = np.ascontiguousarray(x[b])
        m["mem"] = np.ascontiguousarray(mem[b])
        in_maps.append(m)
    res = run_bass_kernel_spmd(nc, in_maps, core_ids=list(range(n)))
    return np.stack([np.asarray(r["out"], dtype=np.float32) for r in res.results], axis=0)
```
